# Optimizing a Trainium2 kernel written in Bass

```python
import math
import jax, jax.numpy as jnp
from jax import lax
import numpy as np

D_MODEL = 1024
BATCH = 2
SEQ = 16384
DEPTH = 1

N_HEADS = 8
HEAD_DIM = 64
V_DIM = 2 * HEAD_DIM
ROPE_THETA = 10000.0
Q_BLOCK = 128
SSM_GROUP = 16
SSM_GROUPS = 32
SSM_WIDTH = SSM_GROUP * SSM_GROUPS
SSM_STATE = 64
DT_MIN = 0.001
DT_MAX = 0.1
D_FF = 2816
CONV_WIDTH = 3
EPS = 1e-6
QK_COLS = N_HEADS * 2 * HEAD_DIM
V_COLS = N_HEADS * V_DIM
GATE_COLS = 2 * D_MODEL
IN_COLS = 2 * QK_COLS + V_COLS + SSM_WIDTH + GATE_COLS

kernel_name = "hybrid_diffattn_s5_gated_convffn"


def rmsnorm(x, g):
    xf = x.astype(jnp.float32)
    xf = xf * lax.rsqrt(jnp.mean(xf * xf, axis=-1, keepdims=True) + EPS)
    return (xf * g.astype(jnp.float32)).astype(x.dtype)


def rope(t, cos, sin):
    tf = t.astype(jnp.float32)
    t1, t2 = jnp.split(tf, 2, axis=-1)
    out = jnp.concatenate([t1 * cos - t2 * sin, t2 * cos + t1 * sin], axis=-1)
    return out.astype(t.dtype)


def diff_attention(q, k, v, lam):
    bsz, seq = q.shape[0], q.shape[1]
    nb = seq // Q_BLOCK
    scale = 1.0 / math.sqrt(HEAD_DIM)
    qb = q.reshape(bsz, nb, Q_BLOCK, N_HEADS, 2, HEAD_DIM).transpose(1, 0, 2, 3, 4, 5)
    kpos = jnp.arange(seq)
    neg = jnp.finfo(jnp.float32).min

    def one_block(args):
        qblk, bi = args
        s = jnp.einsum('bqhcd,bkhcd->bhcqk', qblk, k).astype(jnp.float32) * scale
        qpos = bi * Q_BLOCK + jnp.arange(Q_BLOCK)
        mask = kpos[None, :] <= qpos[:, None]
        s = jnp.where(mask, s, neg)
        p = jax.nn.softmax(s, axis=-1)
        a = p[:, :, 0] - lam * p[:, :, 1]
        return jnp.einsum('bhqk,bkhe->bqhe', a.astype(v.dtype), v)

    o = lax.map(one_block, (qb, jnp.arange(nb)))
    return o.transpose(1, 0, 2, 3, 4).reshape(bsz, seq, N_HEADS, V_DIM)


def _complex_affine_combine(e1, e2):
    a1r, a1i, b1r, b1i = e1
    a2r, a2i, b2r, b2i = e2
    ar = a2r * a1r - a2i * a1i
    ai = a2r * a1i + a2i * a1r
    br = a2r * b1r - a2i * b1i + b2r
    bi = a2r * b1i + a2i * b1r + b2i
    return ar, ai, br, bi


def s5_layer(u, a_re, a_im, log_dt, b_re, b_im, c_re, c_im, d_skip, w_glu, b_glu):
    dtype = u.dtype
    bsz, seq = u.shape[0], u.shape[1]
    f32 = jnp.float32
    uf = u.astype(f32).reshape(bsz, seq, SSM_GROUPS, SSM_GROUP)
    a_re = a_re.astype(f32); a_im = a_im.astype(f32)
    dt = jnp.exp(log_dt.astype(f32))[:, None]
    mag = jnp.exp(a_re * dt)
    lb_re = mag * jnp.cos(a_im * dt)
    lb_im = mag * jnp.sin(a_im * dt)
    nr, ni = lb_re - 1.0, lb_im
    den = a_re * a_re + a_im * a_im
    cr = (nr * a_re + ni * a_im) / den
    ci = (ni * a_re - nr * a_im) / den
    b_re = b_re.astype(f32); b_im = b_im.astype(f32)
    bb_re = cr[..., None] * b_re - ci[..., None] * b_im
    bb_im = cr[..., None] * b_im + ci[..., None] * b_re
    bu_re = jnp.einsum('blgc,gpc->blgp', uf, bb_re)
    bu_im = jnp.einsum('blgc,gpc->blgp', uf, bb_im)
    ar = jnp.broadcast_to(lb_re, bu_re.shape)
    ai = jnp.broadcast_to(lb_im, bu_im.shape)
    _, _, xr, xi = lax.associative_scan(_complex_affine_combine, (ar, ai, bu_re, bu_im), axis=1)
    y = (jnp.einsum('gcp,blgp->blgc', c_re.astype(f32), xr)
         - jnp.einsum('gcp,blgp->blgc', c_im.astype(f32), xi)
         + d_skip.astype(f32) * uf)
    y = y.reshape(bsz, seq, SSM_WIDTH)
    g = jax.nn.gelu(y)
    out = g * jax.nn.sigmoid(g @ w_glu.astype(f32) + b_glu.astype(f32))
    return out.astype(dtype)


def causal_dwconv(u, w, b):
    c = u.shape[-1]
    out = lax.conv_general_dilated(
        u, w.astype(u.dtype)[:, None, :], window_strides=(1,),
        padding=[(CONV_WIDTH - 1, 0)], dimension_numbers=('NWC', 'WIO', 'NWC'),
        feature_group_count=c)
    return out + b.astype(u.dtype)


def setup_inputs(seed: int = 0) -> dict:
    key = jax.random.key(seed)
    ks = iter(jax.random.split(key, 40))

    def nrm(shape, scale):
        return jax.random.normal(next(ks), shape, jnp.float32) * scale

    def gain(shape):
        return 1.0 + nrm(shape, 0.02)

    L = DEPTH
    x = nrm((BATCH, SEQ, D_MODEL), 1.0)
    norm1_gain = gain((L, D_MODEL))
    w_in = nrm((L, D_MODEL, IN_COLS), D_MODEL ** -0.5)
    q_norm_gain = gain((L, HEAD_DIM))
    k_norm_gain = gain((L, HEAD_DIM))
    lambda_q1 = nrm((L, HEAD_DIM), 0.1)
    lambda_k1 = nrm((L, HEAD_DIM), 0.1)
    lambda_q2 = nrm((L, HEAD_DIM), 0.1)
    lambda_k2 = nrm((L, HEAD_DIM), 0.1)
    subln_gain = gain((L, V_DIM))
    w_attn_proj = nrm((L, V_COLS, D_MODEL), V_COLS ** -0.5)
    ssm_a_re = -0.5 * (1.0 + nrm((L, SSM_GROUPS, SSM_STATE), 0.02))
    ssm_a_im = (math.pi * jnp.arange(SSM_STATE, dtype=jnp.float32))[None, None, :] + nrm((L, SSM_GROUPS, SSM_STATE), 0.02)
    ssm_log_dt = jax.random.uniform(next(ks), (L, SSM_GROUPS), jnp.float32,
                                    math.log(DT_MIN), math.log(DT_MAX))
    ssm_b_re = nrm((L, SSM_GROUPS, SSM_STATE, SSM_GROUP), (2 * SSM_GROUP) ** -0.5)
    ssm_b_im = nrm((L, SSM_GROUPS, SSM_STATE, SSM_GROUP), (2 * SSM_GROUP) ** -0.5)
    ssm_c_re = nrm((L, SSM_GROUPS, SSM_GROUP, SSM_STATE), (2 * SSM_STATE) ** -0.5)
    ssm_c_im = nrm((L, SSM_GROUPS, SSM_GROUP, SSM_STATE), (2 * SSM_STATE) ** -0.5)
    ssm_d = nrm((L, SSM_GROUPS, SSM_GROUP), 1.0)
    w_glu = nrm((L, SSM_WIDTH, SSM_WIDTH), SSM_WIDTH ** -0.5)
    b_glu = nrm((L, SSM_WIDTH), 0.02)
    w_ssm_proj = nrm((L, SSM_WIDTH, D_MODEL), SSM_WIDTH ** -0.5)
    w_out = nrm((L, D_MODEL, D_MODEL), D_MODEL ** -0.5)
    norm2_gain = gain((L, D_MODEL))
    w_up = nrm((L, D_MODEL, 2 * D_FF), D_MODEL ** -0.5)
    conv_w = nrm((L, CONV_WIDTH, 2 * D_FF), CONV_WIDTH ** -0.5)
    conv_b = nrm((L, 2 * D_FF), 0.02)
    w_down = nrm((L, D_FF, D_MODEL), D_FF ** -0.5)
    return {"x": x, "norm1_gain": norm1_gain, "w_in": w_in,
            "q_norm_gain": q_norm_gain, "k_norm_gain": k_norm_gain,
            "lambda_q1": lambda_q1, "lambda_k1": lambda_k1,
            "lambda_q2": lambda_q2, "lambda_k2": lambda_k2,
            "subln_gain": subln_gain, "w_attn_proj": w_attn_proj,
            "ssm_a_re": ssm_a_re, "ssm_a_im": ssm_a_im, "ssm_log_dt": ssm_log_dt,
            "ssm_b_re": ssm_b_re, "ssm_b_im": ssm_b_im,
            "ssm_c_re": ssm_c_re, "ssm_c_im": ssm_c_im, "ssm_d": ssm_d,
            "w_glu": w_glu, "b_glu": b_glu, "w_ssm_proj": w_ssm_proj,
            "w_out": w_out, "norm2_gain": norm2_gain, "w_up": w_up,
            "conv_w": conv_w, "conv_b": conv_b, "w_down": w_down}


def reference(x, norm1_gain, w_in, q_norm_gain, k_norm_gain, lambda_q1, lambda_k1,
              lambda_q2, lambda_k2, subln_gain, w_attn_proj, ssm_a_re, ssm_a_im,
              ssm_log_dt, ssm_b_re, ssm_b_im, ssm_c_re, ssm_c_im, ssm_d, w_glu, b_glu,
              w_ssm_proj, w_out, norm2_gain, w_up, conv_w, conv_b, w_down):
    bsz, seq, _ = x.shape
    f32 = jnp.float32
    pos = jnp.arange(seq, dtype=f32)
    inv_freq = 1.0 / (ROPE_THETA ** (jnp.arange(0, HEAD_DIM, 2, dtype=f32) / HEAD_DIM))
    ang = pos[:, None] * inv_freq[None, :]
    cos = jnp.cos(ang)[None, :, None, None, :]
    sin = jnp.sin(ang)[None, :, None, None, :]
    splits = [QK_COLS, 2 * QK_COLS, 2 * QK_COLS + V_COLS, 2 * QK_COLS + V_COLS + SSM_WIDTH]

    h = x
    for i in range(DEPTH):
        lam_init = 0.8 - 0.6 * math.exp(-0.3 * i)
        xn = rmsnorm(h, norm1_gain[i])
        proj = xn @ w_in[i]
        q, k, v, u, gates = jnp.split(proj, splits, axis=-1)
        q = q.reshape(bsz, seq, N_HEADS, 2, HEAD_DIM)
        k = k.reshape(bsz, seq, N_HEADS, 2, HEAD_DIM)
        v = v.reshape(bsz, seq, N_HEADS, V_DIM)
        q = rope(rmsnorm(q, q_norm_gain[i]), cos, sin)
        k = rope(rmsnorm(k, k_norm_gain[i]), cos, sin)
        lam = (jnp.exp(jnp.sum(lambda_q1[i].astype(f32) * lambda_k1[i].astype(f32)))
               - jnp.exp(jnp.sum(lambda_q2[i].astype(f32) * lambda_k2[i].astype(f32)))
               + lam_init)
        o = diff_attention(q, k, v, lam)
        o = rmsnorm(o, subln_gain[i]) * (1.0 - lam_init)
        attn_branch = o.reshape(bsz, seq, V_COLS) @ w_attn_proj[i]
        s = s5_layer(u, ssm_a_re[i], ssm_a_im[i], ssm_log_dt[i], ssm_b_re[i], ssm_b_im[i],
                     ssm_c_re[i], ssm_c_im[i], ssm_d[i], w_glu[i], b_glu[i])
        ssm_branch = s @ w_ssm_proj[i]
        g_attn, g_ssm = jnp.split(jax.nn.sigmoid(gates), 2, axis=-1)
        h = h + (g_attn * attn_branch + g_ssm * ssm_branch) @ w_out[i]
        hn = rmsnorm(h, norm2_gain[i])
        up = causal_dwconv(hn @ w_up[i], conv_w[i], conv_b[i])
        gate, val = jnp.split(up, 2, axis=-1)
        h = h + (jax.nn.silu(gate) * val) @ w_down[i]
    return h
```

```python
import math
from contextlib import ExitStack
import numpy as np
import concourse.bass as bass
import concourse.mybir as mybir
from concourse.bass_utils import run_bass_kernel_spmd

F32 = mybir.dt.float32
BF = mybir.dt.bfloat16
AF = mybir.ActivationFunctionType
ALU = mybir.AluOpType
AX = mybir.AxisListType

D = 1024
WIN = 16384
OWN0 = 12160
NE = WIN - OWN0
T = 512
NT = WIN // T
EPS = 1e-6
LAM_INIT = 0.2
ENGS = ("sp", "act", "dve", "pool", "pe")
LIMIT = 24000
NDMA = 24


class Buf:
    __slots__ = ("t", "lw", "rd")

    def __init__(self, t):
        self.t = t
        self.lw = None
        self.rd = {}

    def __getitem__(self, k):
        return self.t[k]


class Prog:
    def __init__(self, nc, stack):
        self.nc = nc
        self.stack = stack
        self.ops = {e: [] for e in ENGS}
        self.sem = {}
        self.cnt = {}
        self.waited = {e: {} for e in ENGS}
        self.nsem = 0
        self.pesems = set()
        for e in ENGS:
            self._rot(e)
        self.dsems = [self._new("d") for _ in range(NDMA)]
        self.dcnt = [0] * NDMA
        self.di = 0
        self.nops = 0

    def _new(self, pfx):
        self.nsem += 1
        return self.stack.enter_context(self.nc.semaphore(f"{pfx}{self.nsem}"))

    def _rot(self, e):
        self.sem[e] = self._new(e)
        self.cnt[e] = 0
        if e == "pe":
            self.pesems.add(id(self.sem[e]))

    def op(self, eng, fn, r=(), w=(), dma=False):
        deps = []
        for b in r:
            if b.lw is not None:
                deps.append(b.lw)
        for b in w:
            if b.lw is not None:
                deps.append(b.lw)
            deps.extend(b.rd.values())
        if dma:
            i = self.di
            self.di = (self.di + 1) % NDMA
            sem = self.dsems[i]
            if self.dcnt[i] > 0:
                deps.append((sem, self.dcnt[i]))
            self.dcnt[i] += 16
            tok = (sem, self.dcnt[i])
            inc = 16
        else:
            if self.cnt[eng] >= LIMIT:
                self._rot(eng)
            self.cnt[eng] += 1
            tok = (self.sem[eng], self.cnt[eng])
            inc = 1
        m = {}
        for sm, v in deps:
            if eng == "pe" and id(sm) in self.pesems:
                continue
            k = id(sm)
            if k not in m or m[k][1] < v:
                m[k] = (sm, v)
        wd = self.waited[eng]
        waits = []
        for k, (sm, v) in m.items():
            if wd.get(k, 0) < v:
                waits.append((sm, v))
                wd[k] = v
        self.ops[eng].append((fn, waits, tok[0], inc))
        for b in r:
            k = id(tok[0])
            b.rd[k] = tok
        for b in w:
            b.lw = tok
            b.rd = {}
        self.nops += 1
        return tok

    def barrier(self):
        toks = []
        for e in ENGS:
            if self.cnt[e] > 0:
                toks.append((self.sem[e], self.cnt[e]))
        for i in range(NDMA):
            if self.dcnt[i] > 0:
                toks.append((self.dsems[i], self.dcnt[i]))
        for e in ENGS:
            wd = self.waited[e]
            waits = []
            for sm, v in toks:
                if sm is self.sem[e]:
                    continue
                if wd.get(id(sm), 0) < v:
                    waits.append((sm, v))
                    wd[id(sm)] = v
            self.ops[e].append((None, waits, None, 0))

    def emit(self, name):
        nc = self.nc
        with nc.Block(name) as blk:
            for eng, dec in (("sp", blk.sync), ("act", blk.scalar), ("dve", blk.vector),
                             ("pool", blk.gpsimd), ("pe", blk.tensor)):
                ops = self.ops[eng]

                def body(e, ops=ops):
                    for fn, waits, sem, inc in ops:
                        for sm, v in waits:
                            e.wait_ge(sm, v)
                        if fn is not None:
                            fn(e).then_inc(sem, inc)
                dec(body)
        self.ops = {e: [] for e in ENGS}


class Ring:
    def __init__(self, bufs):
        self.b = bufs
        self.i = 0

    def next(self):
        b = self.b[self.i]
        self.i = (self.i + 1) % len(self.b)
        return b


def build_nc(stop_after=5, dbg=False):
    nc = bass.Bass("TRN2", target_bir_lowering=False)

    def din(name, shape, dt=F32):
        return nc.dram_tensor(name, list(shape), dt, kind="ExternalInput").ap()

    def dscr(name, shape, dt):
        return nc.dram_tensor(name, list(shape), dt, kind=("ExternalOutput" if (dbg and name in ("ST", "OT", "HT", "HNT")) else "Internal")).ap()

    xT = din("xT", [D, WIN])
    cosT = din("cosT", [128, WIN])
    sinT = din("sinT", [128, WIN])
    validT = din("validT", [128, 128])
    ident_d = din("ident", [128, 128])
    perm_d = din("perm", [128, 128])
    tri_d = din("tri", [128, 128])
    jrow_d = din("jrow", [128, T])
    g1_d = din("g1", [128, 8])
    g2_d = din("g2", [128, 8])
    gqk_d = din("gqk", [128, 2])
    lamp_d = din("lamp", [128, 4, 64])
    subg_d = din("subg", [128, 128])
    w_in = din("w_in", [D, 5632])
    w_ap = din("w_ap", [D, D])
    w_glu = din("w_glu", [512, 512])
    bglu_d = din("bglu", [128, 4])
    w_sp = din("w_sp", [512, D])
    w_out = din("w_out", [D, D])
    w_up = din("w_up", [D, 5632])
    convw_d = din("convw", [128, 44, 3])
    convb_d = din("convb", [128, 44])
    w_down = din("w_down", [2816, D])
    s5a_d = din("s5a", [128, 3, 16])
    s5b_d = din("s5b", [128, 2, 16, 16])
    s5c_d = din("s5c", [128, 2, 16, 16])
    s5d_d = din("s5d", [128, 4])
    yT = nc.dram_tensor("yT", [D, 4096], F32, kind="ExternalOutput").ap()

    KT = dscr("KT", [8, 128, WIN], BF)
    VS = dscr("VS", [128, 128, 8 * 130], BF)
    QT = dscr("QT", [8, 128, NE], BF)
    UT = dscr("UT", [512, WIN], BF)
    GT = dscr("GT", [2048, NE], BF)
    ST = dscr("ST", [512, NE], BF)
    OT = dscr("OT", [D, NE], BF)
    HT = dscr("HT", [D, NE], F32)
    HNT = dscr("HNT", [D, NE], BF)
    dbg_out = {}

    with ExitStack() as gs:
        P = Prog(nc, gs)

        _cnt = [0]

        def sb(stack, name, shape, dt):
            _cnt[0] += 1
            return Buf(stack.enter_context(nc.sbuf_tensor(f"sb{_cnt[0]}_{name}", list(shape), dt)))

        banks = [Buf(gs.enter_context(nc.psum_tensor(f"pb{i}", [128, 512], F32))) for i in range(8)]

        ones_bf = sb(gs, "ones_bf", [128, 128], BF)
        blk_bf = sb(gs, "blk_bf", [128, 128], BF)
        ident_f = sb(gs, "ident_f", [128, 128], F32)
        ident_b = sb(gs, "ident_b", [128, 128], BF)
        perm_b = sb(gs, "perm_b", [128, 128], BF)
        tri_b = sb(gs, "tri_b", [128, 128], BF)
        g1 = sb(gs, "g1", [128, 8], F32)
        g2 = sb(gs, "g2", [128, 8], F32)
        gqk = sb(gs, "gqk", [128, 2], F32)
        neglam = sb(gs, "neglam", [128, 1], F32)
        subg = sb(gs, "subg", [128, 128], F32)
        valid = sb(gs, "valid", [128, 128], F32)
        epsb = sb(gs, "epsb", [128, 1], F32)

        def dma(eng, out_b, out_ap, in_b, in_ap):
            r = [in_b] if in_b is not None else []
            w = [out_b] if out_b is not None else []
            return P.op(eng, lambda e: e.dma_start(out=out_ap, in_=in_ap), r=r, w=w, dma=True)


        def ACT(out, in_, func, r, w, **kw):
            return P.op("act", lambda e: e.activation(out=out, in_=in_, func=func, **kw), r=r, w=w)

        def MM(out, lhsT, rhs, start, stop, r, w):
            return P.op("pe", lambda e: e.matmul(out, lhsT=lhsT, rhs=rhs, start=start, stop=stop), r=r, w=w)

        def TT(eng, out, in0, in1, op, r, w):
            return P.op(eng, lambda e: e.tensor_tensor(out=out, in0=in0, in1=in1, op=op), r=r, w=w)

        def STT(eng, out, in0, scalar, in1, op0, op1, r, w):
            return P.op(eng, lambda e: e.scalar_tensor_tensor(out=out, in0=in0, scalar=scalar, in1=in1, op0=op0, op1=op1), r=r, w=w)

        def TS(eng, out, in0, s1, s2, op0, op1, r, w):
            if s2 is None:
                return P.op(eng, lambda e: e.tensor_scalar(out=out, in0=in0, scalar1=s1, scalar2=None, op0=op0), r=r, w=w)
            return P.op(eng, lambda e: e.tensor_scalar(out=out, in0=in0, scalar1=s1, scalar2=s2, op0=op0, op1=op1), r=r, w=w)

        def RECIP(out, in_, r, w):
            return P.op("dve", lambda e: e.reciprocal(out=out, in_=in_), r=r, w=w)

        def COPY(eng, out, in_, r, w):
            return P.op(eng, lambda e: e.tensor_copy(out=out, in_=in_), r=r, w=w)

        def DMA(eng, out_ap, in_ap, r=(), w=()):
            return P.op(eng, lambda e: e.dma_start(out=out_ap, in_=in_ap), r=list(r), w=list(w), dma=True)

        with ExitStack() as ps:
            lamp = sb(ps, "lamp", [128, 4, 64], F32)
            lt = sb(ps, "lt", [128, 2, 64], F32)
            ls = sb(ps, "ls", [128, 2], F32)
            P.op("dve", lambda e: e.memset(ones_bf[:], 1.0), w=[ones_bf])
            P.op("dve", lambda e: e.memset(epsb[:], EPS), w=[epsb])
            P.op("dve", lambda e: e.memset(blk_bf[:], 0.0), w=[blk_bf])
            P.op("dve", lambda e: e.memset(blk_bf[0:64, 0:64], 1.0), w=[blk_bf])
            P.op("dve", lambda e: e.memset(blk_bf[64:128, 64:128], 1.0), w=[blk_bf])
            dma("sp", ident_f, ident_f[:], None, ident_d[:, :])
            dma("pool", ident_b, ident_b[:], None, ident_d[:, :])
            dma("pool", perm_b, perm_b[:], None, perm_d[:, :])
            dma("pool", tri_b, tri_b[:], None, tri_d[:, :])
            dma("sp", g1, g1[:], None, g1_d[:, :])
            dma("sp", g2, g2[:], None, g2_d[:, :])
            dma("sp", gqk, gqk[:], None, gqk_d[:, :])
            dma("sp", subg, subg[:], None, subg_d[:, :])
            dma("sp", valid, valid[:], None, validT[:, :])
            dma("sp", lamp, lamp[:], None, lamp_d[:, :, :])
            P.op("dve", lambda e: e.tensor_tensor(out=lt[:, 0, :], in0=lamp[:, 0, :], in1=lamp[:, 1, :], op=ALU.mult), r=[lamp], w=[lt])
            P.op("dve", lambda e: e.tensor_tensor(out=lt[:, 1, :], in0=lamp[:, 2, :], in1=lamp[:, 3, :], op=ALU.mult), r=[lamp], w=[lt])
            P.op("dve", lambda e: e.tensor_reduce(out=ls[:], in_=lt[:], axis=AX.X, op=ALU.add), r=[lt], w=[ls])
            P.op("act", lambda e: e.activation(out=ls[:], in_=ls[:], func=AF.Exp), r=[ls], w=[ls])
            P.op("dve", lambda e: e.tensor_tensor(out=neglam[:], in0=ls[:, 1:2], in1=ls[:, 0:1], op=ALU.subtract), r=[ls], w=[neglam])
            P.op("dve", lambda e: e.tensor_scalar(out=neglam[:], in0=neglam[:], scalar1=-LAM_INIT, scalar2=None, op0=ALU.add), r=[neglam], w=[neglam])
            P.op("dve", lambda e: e.tensor_scalar(out=gqk[:, 0:1], in0=gqk[:, 0:1], scalar1=0.125, scalar2=None, op0=ALU.mult), r=[gqk], w=[gqk])
            P.barrier()
            P.emit("ph0")

        if stop_after >= 1:
            with ExitStack() as ps:
                Wq = sb(ps, "Wq", [128, 8, 1024], BF)
                Wk = sb(ps, "Wk", [128, 8, 1024], BF)
                Wv = sb(ps, "Wv", [128, 8, 1024], BF)
                Wu = sb(ps, "Wu", [128, 8, 512], BF)
                Wg = sb(ps, "Wg", [128, 8, 2048], BF)
                w_in_v = w_in.rearrange("(c p) n -> p c n", p=128)
                for c in range(8):
                    DMA("pool", Wk[:, c, :], w_in_v[:, c, 1024:2048], w=[Wk])
                    DMA("pool", Wv[:, c, :], w_in_v[:, c, 2048:3072], w=[Wv])
                    DMA("pool", Wu[:, c, :], w_in_v[:, c, 3072:3584], w=[Wu])
                for c in range(8):
                    DMA("pool", Wq[:, c, :], w_in_v[:, c, 0:1024], w=[Wq])
                    DMA("pool", Wg[:, c, :], w_in_v[:, c, 3584:5632], w=[Wg])
                xr = Ring([sb(ps, f"x{i}", [128, 8, T], F32) for i in range(2)])
                csr = Ring([sb(ps, f"cs{i}", [128, 2, T], F32) for i in range(2)])
                sq = sb(ps, "sq", [128, 8, T], BF)
                xnr = Ring([sb(ps, f"xn{i}", [128, 8, T], BF) for i in range(2)])
                sdr = Ring([sb(ps, f"sd{i}", [128, T], F32) for i in range(3)])
                sqh = Ring([sb(ps, f"sqh{i}", [128, T], BF) for i in range(2)])
                kgr = Ring([sb(ps, f"kg{i}", [128, T], BF) for i in range(2)])
                t1r = Ring([sb(ps, f"t1{i}", [128, T], F32) for i in range(2)])
                t2r = Ring([sb(ps, f"t2{i}", [128, T], F32) for i in range(2)])
                kor = Ring([sb(ps, f"ko{i}", [128, T], BF) for i in range(3)])
                vor = Ring([sb(ps, f"vo{i}", [128, 8, 130], BF) for i in range(3)])
                uor = Ring([sb(ps, f"uo{i}", [128, T], BF) for i in range(3)])
                xT_v = xT.rearrange("(c p) t -> p c t", p=128)
                UT_v = UT.rearrange("(q p) t -> p q t", p=128)
                GT_v = GT.rearrange("(m p) t -> p m t", p=128)
                pr = Ring(banks)

                def qk_post(ps_b, gcol, cs, c0, c1, dst_ap):
                    n = c1 - c0
                    s_ = sqh.next()
                    ACT(s_[:, 0:n], ps_b[:, c0:c1], AF.Square, r=[ps_b], w=[s_])
                    ssb = pr.next()
                    MM(ssb[:, 0:n], blk_bf[:], s_[:, 0:n], True, True, r=[blk_bf, s_], w=[ssb])
                    sd = sdr.next()
                    ACT(sd[:, 0:n], ssb[:, 0:n], AF.Sqrt, r=[ssb], w=[sd], scale=1.0 / 64.0, bias=epsb[:])
                    RECIP(sd[:, 0:n], sd[:, 0:n], r=[sd], w=[sd])
                    kg = kgr.next()
                    ACT(kg[:, 0:n], ps_b[:, c0:c1], AF.Copy, r=[ps_b, gqk], w=[kg], scale=gqk[:, gcol:gcol + 1])
                    swb = pr.next()
                    MM(swb[:, 0:n], perm_b[:], kg[:, 0:n], True, True, r=[perm_b, kg], w=[swb])
                    t1 = t1r.next()
                    STT("dve", t1[:, 0:n], ps_b[:, c0:c1], gqk[:, gcol:gcol + 1], cs[:, 0, c0:c1], ALU.mult, ALU.mult, r=[ps_b, gqk, cs], w=[t1])
                    t2 = t2r.next()
                    TT("dve", t2[:, 0:n], swb[:, 0:n], cs[:, 1, c0:c1], ALU.mult, r=[swb, cs], w=[t2])
                    TT("pool", t1[:, 0:n], t1[:, 0:n], t2[:, 0:n], ALU.add, r=[t1, t2], w=[t1])
                    ko = kor.next()
                    TT("dve", ko[:, 0:n], t1[:, 0:n], sd[:, 0:n], ALU.mult, r=[t1, sd], w=[ko])
                    DMA("sp", dst_ap, ko[:, 0:n], r=[ko])

                SECT = 255
                for tt in range(NT):
                    t0 = tt * T
                    x = xr.next()
                    for c in range(0, 8, 4):
                        DMA("sp", x[:, c:c + 4, :], xT_v[:, c:c + 4, t0:t0 + T], w=[x])
                    cs = csr.next()
                    DMA("sp", cs[:, 0, :], cosT[:, t0:t0 + T], w=[cs])
                    DMA("sp", cs[:, 1, :], sinT[:, t0:t0 + T], w=[cs])
                    ACT(sq[:], x[:], AF.Square, r=[x], w=[sq])
                    ssb = pr.next()
                    for c in range(8):
                        MM(ssb[:], ones_bf[:], sq[:, c, :], c == 0, c == 7, r=[ones_bf, sq], w=[ssb])
                    sd = sdr.next()
                    ACT(sd[:], ssb[:], AF.Sqrt, r=[ssb], w=[sd], scale=1.0 / D, bias=epsb[:])
                    RECIP(sd[:], sd[:], r=[sd], w=[sd])
                    xn = xnr.next()
                    for c in range(8):
                        STT("dve", xn[:, c, :], x[:, c, :], g1[:, c:c + 1], sd[:], ALU.mult, ALU.mult, r=[x, g1, sd], w=[xn])
                    for h in range(8 if SECT & 2 else 0):
                        kb = pr.next()
                        for c in range(8):
                            MM(kb[:], Wk[:, c, h * 128:(h + 1) * 128], xn[:, c, :], c == 0, c == 7, r=[Wk, xn], w=[kb])
                        qk_post(kb, 1, cs, 0, T, KT[h, :, t0:t0 + T])
                    for j in range(4 if SECT & 4 else 0):
                        vo = vor.next()
                        kt = tt * 4 + j
                        for half in range(2):
                            vb = pr.next()
                            for c in range(8):
                                MM(vb[:], xn[:, c, j * 128:(j + 1) * 128], Wv[:, c, half * 512:(half + 1) * 512], c == 0, c == 7, r=[Wv, xn], w=[vb])
                            ACT(vo[:, half * 4:half * 4 + 4, 0:128], vb[:].rearrange("p (h e) -> p h e", h=4), AF.Copy, r=[vb], w=[vo])
                        COPY("dve", vo[:, :, 128:130], valid[:, kt:kt + 1].to_broadcast([128, 8, 2]), r=[valid], w=[vo])
                        DMA("sp", VS[kt, :, :], vo[:].rearrange("p h e -> p (h e)"), r=[vo])
                    for q in range(4 if SECT & 8 else 0):
                        ub = pr.next()
                        for c in range(8):
                            MM(ub[:], Wu[:, c, q * 128:(q + 1) * 128], xn[:, c, :], c == 0, c == 7, r=[Wu, xn], w=[ub])
                        uo = uor.next()
                        ACT(uo[:], ub[:], AF.Copy, r=[ub], w=[uo])
                        DMA("sp", UT_v[:, q, t0:t0 + T], uo[:], r=[uo])
                    if t0 + T > OWN0:
                        c0 = max(OWN0 - t0, 0)
                        n = T - c0
                        e0 = t0 + c0 - OWN0
                        for h in range(8):
                            qb = pr.next()
                            for c in range(8):
                                MM(qb[:, c0:T], Wq[:, c, h * 128:(h + 1) * 128], xn[:, c, c0:T], c == 0, c == 7, r=[Wq, xn], w=[qb])
                            qk_post(qb, 0, cs, c0, T, QT[h, :, e0:e0 + n])
                        for m in range(16):
                            gb = pr.next()
                            for c in range(8):
                                MM(gb[:, c0:T], Wg[:, c, m * 128:(m + 1) * 128], xn[:, c, c0:T], c == 0, c == 7, r=[Wg, xn], w=[gb])
                            uo = uor.next()
                            ACT(uo[:, 0:n], gb[:, c0:T], AF.Sigmoid, r=[gb], w=[uo])
                            DMA("sp", GT_v[:, m, e0:e0 + n], uo[:, 0:n], r=[uo])
                P.barrier()
                P.emit("ph1")
        if stop_after >= 2:
            with ExitStack() as ps:
                s5a = sb(ps, "s5a", [128, 3, 16], F32)
                s5b = sb(ps, "s5b", [128, 2, 16, 16], F32)
                s5c = sb(ps, "s5c", [128, 2, 16, 16], F32)
                s5d = sb(ps, "s5d", [128, 4], F32)
                bglu = sb(ps, "bglu", [128, 4], F32)
                jrow = sb(ps, "jrow", [128, T], F32)
                twopi = sb(ps, "twopi", [128, T], F32)
                negpi = sb(ps, "negpi", [128, 1], F32)
                sm = [sb(ps, f"sm{i}", [128, 16], F32) for i in range(12)]
                cosJ = sb(ps, "cosJ", [128, 16, T], F32)
                sinJ = sb(ps, "sinJ", [128, 16, T], F32)
                bb = sb(ps, "bb", [128, 2, 16, 16], F32)
                bd = sb(ps, "bd", [128, 16, 128], F32)
                Bf = [sb(ps, f"Bf{i}", [128, 16, 128], BF) for i in range(2)]
                Cf = [sb(ps, f"Cf{i}", [128, 16, 128], BF) for i in range(2)]
                Wgl = sb(ps, "Wgl", [128, 4, 512], BF)
                w0 = [sb(ps, f"w0{i}", [128, 16], F32) for i in range(2)]
                tmp4 = sb(ps, "tmp4", [128, 4], F32)
                DMA("sp", s5a[:], s5a_d[:, :, :], w=[s5a])
                DMA("sp", s5b[:], s5b_d[:, :, :, :], w=[s5b])
                DMA("sp", s5c[:], s5c_d[:, :, :, :], w=[s5c])
                DMA("sp", s5d[:], s5d_d[:, :], w=[s5d])
                DMA("sp", bglu[:], bglu_d[:, :], w=[bglu])
                DMA("sp", jrow[:], jrow_d[:, :], w=[jrow])
                DMA("pool", Wgl[:], w_glu.rearrange("(q p) n -> p q n", p=128), w=[Wgl])
                P.op("dve", lambda e: e.memset(twopi[:], 2.0 * math.pi), w=[twopi])
                P.op("dve", lambda e: e.memset(negpi[:], -math.pi), w=[negpi])
                P.op("dve", lambda e: e.memset(w0[0][:], 0.0), w=[w0[0]])
                P.op("dve", lambda e: e.memset(w0[1][:], 0.0), w=[w0[1]])
                dt_, ardt, th, mag, lbre, lbim, den, cr, ci, ta, tb, tc = sm
                AR, AI, LDT = s5a[:, 0, :], s5a[:, 1, :], s5a[:, 2, :]
                ACT(dt_[:], LDT, AF.Exp, r=[s5a], w=[dt_])
                TT("dve", ardt[:], AR, dt_[:], ALU.mult, r=[s5a, dt_], w=[ardt])
                TT("dve", th[:], AI, dt_[:], ALU.mult, r=[s5a, dt_], w=[th])
                ACT(mag[:], ardt[:], AF.Exp, r=[ardt], w=[mag])
                rq = sb(ps, "rq", [128, T], F32)
                rm = sb(ps, "rm", [128, T], F32)
                ki = sb(ps, "ki", [128, T], mybir.dt.int32)
                for sp in range(16):
                    for tab, off in ((sinJ, 0.0), (cosJ, 0.5 * math.pi)):
                        xv = tab[:, sp, :]
                        TS("dve", xv, jrow[:], th[:, sp:sp + 1], off, ALU.mult, ALU.add, r=[jrow, th], w=[tab])
                        TS("dve", rq[:], xv, 1.0 / (2.0 * math.pi), None, ALU.mult, None, r=[tab], w=[rq])
                        COPY("dve", ki[:], rq[:], r=[rq], w=[ki])
                        COPY("dve", rq[:], ki[:], r=[ki], w=[rq])
                        STT("dve", xv, rq[:], -2.0 * math.pi, xv, ALU.mult, ALU.add, r=[rq, tab], w=[tab])
                        TS("dve", rm[:], xv, math.pi, -2.0 * math.pi, ALU.is_gt, ALU.mult, r=[tab], w=[rm])
                        TT("dve", xv, xv, rm[:], ALU.add, r=[tab, rm], w=[tab])
                        TS("dve", rm[:], xv, -math.pi, 2.0 * math.pi, ALU.is_lt, ALU.mult, r=[tab], w=[rm])
                        TT("dve", xv, xv, rm[:], ALU.add, r=[tab, rm], w=[tab])
                for tab in (sinJ, cosJ):
                    for half in range(2):
                        ACT(tab[:, half * 8:half * 8 + 8, :], tab[:, half * 8:half * 8 + 8, :], AF.Sin, r=[tab], w=[tab])
                cos1, sin1 = cosJ[:, :, 0], sinJ[:, :, 0]
                TT("dve", lbre[:], mag[:], cos1, ALU.mult, r=[mag, cosJ], w=[lbre])
                TT("dve", lbim[:], mag[:], sin1, ALU.mult, r=[mag, sinJ], w=[lbim])
                TS("dve", lbre[:], lbre[:], -1.0, None, ALU.add, None, r=[lbre], w=[lbre])
                TT("dve", den[:], AR, AR, ALU.mult, r=[s5a], w=[den])
                TT("dve", ta[:], AI, AI, ALU.mult, r=[s5a], w=[ta])
                TT("dve", den[:], den[:], ta[:], ALU.add, r=[den, ta], w=[den])
                RECIP(den[:], den[:], r=[den], w=[den])
                TT("dve", ta[:], lbre[:], AR, ALU.mult, r=[lbre, s5a], w=[ta])
                TT("dve", tb[:], lbim[:], AI, ALU.mult, r=[lbim, s5a], w=[tb])
                TT("dve", ta[:], ta[:], tb[:], ALU.add, r=[ta, tb], w=[ta])
                TT("dve", cr[:], ta[:], den[:], ALU.mult, r=[ta, den], w=[cr])
                TT("dve", ta[:], lbim[:], AR, ALU.mult, r=[lbim, s5a], w=[ta])
                TT("dve", tb[:], lbre[:], AI, ALU.mult, r=[lbre, s5a], w=[tb])
                TT("dve", ta[:], ta[:], tb[:], ALU.subtract, r=[ta, tb], w=[ta])
                TT("dve", ci[:], ta[:], den[:], ALU.mult, r=[ta, den], w=[ci])
                crb = cr[:].unsqueeze(2).to_broadcast([128, 16, 16])
                cib = ci[:].unsqueeze(2).to_broadcast([128, 16, 16])
                t16a = sb(ps, "t16a", [128, 16, 16], F32)
                BRE, BIM = s5b[:, 0, :, :], s5b[:, 1, :, :]
                TT("dve", bb[:, 0, :, :], BRE, crb, ALU.mult, r=[s5b, cr], w=[bb])
                TT("dve", t16a[:], BIM, cib, ALU.mult, r=[s5b, ci], w=[t16a])
                TT("dve", bb[:, 0, :, :], bb[:, 0, :, :], t16a[:], ALU.subtract, r=[bb, t16a], w=[bb])
                TT("dve", bb[:, 1, :, :], BIM, crb, ALU.mult, r=[s5b, cr, bb], w=[bb])
                TT("dve", t16a[:], BRE, cib, ALU.mult, r=[s5b, ci, bb], w=[t16a])
                TT("dve", bb[:, 1, :, :], bb[:, 1, :, :], t16a[:], ALU.add, r=[bb, t16a], w=[bb])
                pr = Ring(banks[2:8])
                def place(dst, src3, neg=False):
                    P.op("dve", lambda e: e.memset(dst[:], 0.0), w=[dst])
                    for a in range(2):
                        for s_ in range(4):
                            o_ap = dst[64 * a:64 * a + 64, s_:16:4, 32 * s_ + 16 * a:32 * s_ + 16 * a + 16]
                            i_ap = src3[64 * a:64 * a + 64, s_:16:4, :]
                            if neg:
                                TS("dve", o_ap, i_ap, -1.0, None, ALU.mult, None, r=[s5c, bb], w=[dst])
                            else:
                                COPY("dve", o_ap, i_ap, r=[s5c, bb], w=[dst])
                for i in range(2):
                    place(bd, bb[:, i, :, :])
                    for sp in range(16):
                        tb_ = pr.next()
                        P.op("pe", lambda e, tb_=tb_, sp=sp: e.transpose(tb_[:, 0:128], bd[:, sp, :], ident_f[:]), r=[bd, ident_f], w=[tb_])
                        ACT(Bf[i][:, sp, :], tb_[:, 0:128], AF.Copy, r=[tb_], w=[Bf[i]])
                place(Cf[0], s5c[:, 0, :, :])
                place(Cf[1], s5c[:, 1, :, :], neg=True)

                ur = Ring([sb(ps, f"u{i}", [128, 4, T], BF) for i in range(2)])
                tr = Ring([sb(ps, f"tq{i}", [128, T], F32) for i in range(8)])
                hr = Ring([sb(ps, f"hh{i}", [128, T], F32) for i in range(4)])
                wr = Ring([sb(ps, f"ww{i}", [128, T], F32) for i in range(6)])
                xr2 = Ring([sb(ps, f"xx{i}", [128, T], BF) for i in range(4)])
                yvr = Ring([sb(ps, f"yv{i}", [128, T], F32) for i in range(2)])
                g4r = Ring([sb(ps, f"g4{i}", [128, 4, T], BF) for i in range(2)])
                sor = Ring([sb(ps, f"so{i}", [128, T], BF) for i in range(2)])
                tnr = Ring([sb(ps, f"tn{i}", [128, 4], F32) for i in range(4)])
                UT_v = UT.rearrange("(q p) t -> p q t", p=128)
                ST_v = ST.rearrange("(q p) t -> p q t", p=128)
                ybk = Ring(banks[0:2])
                for ck in range(NT):
                    t0 = ck * T
                    u = ur.next()
                    DMA("sp", u[:], UT_v[:, :, t0:t0 + T], w=[u])
                    own = t0 + T > OWN0
                    c0 = max(OWN0 - t0, 0)
                    n = T - c0
                    e0 = t0 + c0 - OWN0
                    g4 = g4r.next() if own else None
                    for q in range(4):
                        yb = ybk.next() if own else None
                        for s_ in range(4):
                            sp = 4 * q + s_
                            bre = pr.next()
                            MM(bre[:], Bf[0][:, sp, :], u[:, q, :], True, True, r=[Bf[0], u], w=[bre])
                            bim = pr.next()
                            MM(bim[:], Bf[1][:, sp, :], u[:, q, :], True, True, r=[Bf[1], u], w=[bim])
                            t1, t2, t3, t4 = tr.next(), tr.next(), tr.next(), tr.next()
                            TT("dve", t1[:], bre[:], cosJ[:, sp, :], ALU.mult, r=[bre, cosJ], w=[t1])
                            TT("dve", t2[:], bim[:], sinJ[:, sp, :], ALU.mult, r=[bim, sinJ], w=[t2])
                            TT("dve", t3[:], bim[:], cosJ[:, sp, :], ALU.mult, r=[bim, cosJ], w=[t3])
                            TT("dve", t4[:], bre[:], sinJ[:, sp, :], ALU.mult, r=[bre, sinJ], w=[t4])
                            hre, him = hr.next(), hr.next()
                            TT("pool", hre[:], t1[:], t2[:], ALU.add, r=[t1, t2], w=[hre])
                            TT("pool", him[:], t3[:], t4[:], ALU.subtract, r=[t3, t4], w=[him])
                            wre, wim = wr.next(), wr.next()
                            rho = mag[:, sp:sp + 1].to_broadcast([128, T])
                            P.op("dve", lambda e, wre=wre, hre=hre, sp=sp, rho=rho: e.tensor_tensor_scan(out=wre[:], data0=rho, data1=hre[:], initial=w0[0][:, sp:sp + 1], op0=ALU.mult, op1=ALU.add), r=[mag, hre, w0[0]], w=[wre])
                            P.op("dve", lambda e, wim=wim, him=him, sp=sp, rho=rho: e.tensor_tensor_scan(out=wim[:], data0=rho, data1=him[:], initial=w0[1][:, sp:sp + 1], op0=ALU.mult, op1=ALU.add), r=[mag, him, w0[1]], w=[wim])
                            tn = tnr.next()
                            L = T - 1
                            TT("dve", tn[:, 0:1], wre[:, L:T], cosJ[:, sp, L:T], ALU.mult, r=[wre, cosJ], w=[tn])
                            TT("dve", tn[:, 1:2], wim[:, L:T], sinJ[:, sp, L:T], ALU.mult, r=[wim, sinJ], w=[tn])
                            TT("dve", tn[:, 2:3], wre[:, L:T], sinJ[:, sp, L:T], ALU.mult, r=[wre, sinJ], w=[tn])
                            TT("dve", tn[:, 3:4], wim[:, L:T], cosJ[:, sp, L:T], ALU.mult, r=[wim, cosJ], w=[tn])
                            TT("dve", w0[0][:, sp:sp + 1], tn[:, 0:1], tn[:, 1:2], ALU.subtract, r=[tn], w=[w0[0]])
                            TT("dve", w0[1][:, sp:sp + 1], tn[:, 2:3], tn[:, 3:4], ALU.add, r=[tn], w=[w0[1]])
                            if own:
                                a1, a2, a3, a4 = tr.next(), tr.next(), tr.next(), tr.next()
                                TT("pool", a1[:, c0:T], wre[:, c0:T], cosJ[:, sp, c0:T], ALU.mult, r=[wre, cosJ], w=[a1])
                                TT("pool", a2[:, c0:T], wim[:, c0:T], sinJ[:, sp, c0:T], ALU.mult, r=[wim, sinJ], w=[a2])
                                TT("pool", a3[:, c0:T], wre[:, c0:T], sinJ[:, sp, c0:T], ALU.mult, r=[wre, sinJ], w=[a3])
                                TT("pool", a4[:, c0:T], wim[:, c0:T], cosJ[:, sp, c0:T], ALU.mult, r=[wim, cosJ], w=[a4])
                                xre, xim = xr2.next(), xr2.next()
                                TT("pool", xre[:, c0:T], a1[:, c0:T], a2[:, c0:T], ALU.subtract, r=[a1, a2], w=[xre])
                                TT("pool", xim[:, c0:T], a3[:, c0:T], a4[:, c0:T], ALU.add, r=[a3, a4], w=[xim])
                                MM(yb[:, c0:T], Cf[0][:, sp, :], xre[:, c0:T], s_ == 0, False, r=[Cf[0], xre], w=[yb])
                                MM(yb[:, c0:T], Cf[1][:, sp, :], xim[:, c0:T], False, s_ == 3, r=[Cf[1], xim], w=[yb])
                        if own:
                            yv = yvr.next()
                            STT("dve", yv[:, c0:T], u[:, q, c0:T], s5d[:, q:q + 1], yb[:, c0:T], ALU.mult, ALU.add, r=[u, s5d, yb], w=[yv])
                            i1 = tr.next()
                            ACT(i1[:, c0:T], yv[:, c0:T], AF.Square, r=[yv], w=[i1])
                            TS("dve", i1[:, c0:T], i1[:, c0:T], 0.044715, 1.0, ALU.mult, ALU.add, r=[i1], w=[i1])
                            TT("pool", i1[:, c0:T], i1[:, c0:T], yv[:, c0:T], ALU.mult, r=[i1, yv], w=[i1])
                            ACT(i1[:, c0:T], i1[:, c0:T], AF.Sigmoid, r=[i1], w=[i1], scale=1.5957691216057308)
                            TT("pool", g4[:, q, c0:T], yv[:, c0:T], i1[:, c0:T], ALU.mult, r=[yv, i1], w=[g4])
                    if own:
                        for m in range(4):
                            zb = pr.next()
                            for q in range(4):
                                MM(zb[:, c0:T], Wgl[:, q, m * 128:(m + 1) * 128], g4[:, q, c0:T], q == 0, q == 3, r=[Wgl, g4], w=[zb])
                            z1 = tr.next()
                            ACT(z1[:, c0:T], zb[:, c0:T], AF.Sigmoid, r=[zb, bglu], w=[z1], bias=bglu[:, m:m + 1])
                            so = sor.next()
                            TT("pool", so[:, c0:T], g4[:, m, c0:T], z1[:, c0:T], ALU.mult, r=[g4, z1], w=[so])
                            DMA("sp", ST_v[:, m, e0:e0 + n], so[:, c0:T], r=[so])
                P.barrier()
                P.emit("ph2")
        BLOCKS = [(0, 128)] + [(128 + 512 * i, 512) for i in range(8)]
        if stop_after >= 3:
            with ExitStack() as ps:
                Kh = sb(ps, "Kh", [128, WIN], BF)
                Vh = sb(ps, "Vh", [128, 128, 130], BF)
                Qh = sb(ps, "Qh", [128, NE], BF)
                ptr = Ring([sb(ps, f"pt{i}", [128, T], BF) for i in range(4)])
                onr = [sb(ps, f"on{i}", [128, 4, 128], F32) for i in range(2)]
                dif = sb(ps, "dif", [128, 4, 128], F32)
                junk = sb(ps, "junk", [128, 128], F32)
                sc = Ring([sb(ps, f"sc{i}", [128, 4], F32) for i in range(4)])
                otm = Ring([sb(ps, f"otm{i}", [128, 128], BF) for i in range(2)])
                otr = Ring([sb(ps, f"otr{i}", [128, T], BF) for i in range(2)])
                sring = Ring(banks[0:3])
                tbank = banks[3]
                tb_bf = tbank[:].bitcast(BF)
                oacc = [Buf(banks[4 + s][:, 0:130]) for s in range(4)]
                VS_v = VS.rearrange("k p e -> p k e")
                for h in range(8):
                    for i in range(4):
                        DMA("sp", Kh[:, i * 4096:(i + 1) * 4096], KT[h, :, i * 4096:(i + 1) * 4096], w=[Kh])
                    for i in range(8):
                        DMA("sp", Vh[:, i * 16:(i + 1) * 16, :], VS_v[:, i * 16:(i + 1) * 16, h * 130:(h + 1) * 130], w=[Vh])
                    DMA("sp", Qh[:], QT[h, :, :], w=[Qh])
                    for (e0, Wd) in BLOCKS:
                        qs = OWN0 + e0
                        nsub = Wd // 128
                        nk = (qs + Wd) // 128
                        for c in range(2):
                            accs = oacc
                            for kt in range(nk):
                                col0 = max(kt * 128 - qs, 0)
                                S = sring.next()
                                MM(S[:, col0:Wd], Kh[64 * c:64 * c + 64, kt * 128:(kt + 1) * 128], Qh[64 * c:64 * c + 64, e0 + col0:e0 + Wd], True, True, r=[Kh, Qh], w=[S])
                                Pt = ptr.next()
                                ACT(Pt[:, col0:Wd], S[:, col0:Wd], AF.Exp, r=[S], w=[Pt])
                                if kt * 128 >= qs:
                                    TT("pool", Pt[:, col0:col0 + 128], Pt[:, col0:col0 + 128], tri_b[:], ALU.mult, r=[Pt, tri_b], w=[Pt])
                                for s in range(col0 // 128, nsub):
                                    MM(accs[s][:, :], Pt[:, s * 128:(s + 1) * 128], Vh[:, kt, :], kt == 0, kt == qs // 128 + s, r=[Pt, Vh], w=[accs[s]])
                            sc_ = sc.next()
                            for s in range(nsub):
                                TS("dve", sc_[:, s:s + 1], accs[s][:, 128:129], 1e-30, None, ALU.add, None, r=[accs[s]], w=[sc_])
                            RECIP(sc_[:, 0:nsub], sc_[:, 0:nsub], r=[sc_], w=[sc_])
                            for s in range(nsub):
                                TS("dve", onr[c][:, s, :], accs[s][:, 0:128], sc_[:, s:s + 1], None, ALU.mult, None, r=[accs[s], sc_], w=[onr[c]])
                        STT("dve", dif[:, 0:nsub, :], onr[1][:, 0:nsub, :], neglam[:, 0:1], onr[0][:, 0:nsub, :], ALU.mult, ALU.add, r=[onr[0], onr[1], neglam], w=[dif])
                        sc_ = sc.next()
                        P.op("dve", lambda e, sc_=sc_: e.memset(sc_[:], 0.0), w=[sc_])
                        for s in range(nsub):
                            P.op("act", lambda e, s=s, sc_=sc_: e.activation(out=junk[:], in_=dif[:, s, :], func=AF.Square, accum_out=sc_[:, s:s + 1]), r=[dif], w=[junk, sc_])
                        ACT(sc_[:, 0:nsub], sc_[:, 0:nsub], AF.Sqrt, r=[sc_], w=[sc_], scale=1.0 / 128.0, bias=epsb[:])
                        RECIP(sc_[:, 0:nsub], sc_[:, 0:nsub], r=[sc_], w=[sc_])
                        TS("dve", sc_[:, 0:nsub], sc_[:, 0:nsub], 1.0 - LAM_INIT, None, ALU.mult, None, r=[sc_], w=[sc_])
                        ot = otr.next()
                        for s in range(nsub):
                            om = otm.next()
                            STT("dve", om[:], dif[:, s, :], sc_[:, s:s + 1], subg[:], ALU.mult, ALU.mult, r=[dif, sc_, subg], w=[om])
                            P.op("pe", lambda e, om=om: e.transpose(tb_bf[:, 0:128], om[:], ident_b[:]), r=[om, ident_b], w=[tbank])
                            ACT(ot[:, s * 128:(s + 1) * 128], tb_bf[:, 0:128], AF.Copy, r=[tbank], w=[ot])
                        DMA("sp", OT[h * 128:(h + 1) * 128, e0:e0 + Wd], ot[:, 0:Wd], r=[ot])
                P.barrier()
                P.emit("ph3")
        xT_v = xT.rearrange("(c p) t -> p c t", p=128)
        OT_v = OT.rearrange("(c p) t -> p c t", p=128)
        ST_v = ST.rearrange("(c p) t -> p c t", p=128)
        GT_v = GT.rearrange("(c p) t -> p c t", p=128)
        HT_v = HT.rearrange("(c p) t -> p c t", p=128)
        HNT_v = HNT.rearrange("(c p) t -> p c t", p=128)
        yT_v = yT.rearrange("(c p) t -> p c t", p=128)
        if stop_after >= 4:
            with ExitStack() as ps:
                Wap = sb(ps, "Wap", [128, 8, 1024], BF)
                Wsp = sb(ps, "Wsp", [128, 4, 1024], BF)
                Wout = sb(ps, "Wout", [128, 8, 1024], BF)
                DMA("pool", Wap[:], w_ap.rearrange("(c p) n -> p c n", p=128), w=[Wap])
                DMA("pool", Wsp[:], w_sp.rearrange("(c p) n -> p c n", p=128), w=[Wsp])
                DMA("pool", Wout[:], w_out.rearrange("(c p) n -> p c n", p=128), w=[Wout])
                o_r = Ring([sb(ps, f"o{i}", [128, 8, T], BF) for i in range(2)])
                s_r = Ring([sb(ps, f"s{i}", [128, 4, T], BF) for i in range(2)])
                g_r = Ring([sb(ps, f"g{i}", [128, 16, T], BF) for i in range(2)])
                x_r = Ring([sb(ps, f"xa{i}", [128, 8, T], F32) for i in range(2)])
                mix = sb(ps, "mix", [128, 8, T], BF)
                hh = Ring([sb(ps, f"h{i}", [128, 8, T], F32) for i in range(2)])
                hn_r = Ring([sb(ps, f"hn{i}", [128, 8, T], BF) for i in range(2)])
                sq = sb(ps, "sq4", [128, 8, T], BF)
                t_r = Ring([sb(ps, f"ta{i}", [128, T], F32) for i in range(4)])
                sd_r = Ring([sb(ps, f"sda{i}", [128, T], F32) for i in range(2)])
                pr = Ring(banks)
                for (e0, Wd) in BLOCKS:
                    o, s_, gt, x = o_r.next(), s_r.next(), g_r.next(), x_r.next()
                    DMA("sp", o[:, :, 0:Wd], OT_v[:, :, e0:e0 + Wd], w=[o])
                    DMA("sp", s_[:, :, 0:Wd], ST_v[:, :, e0:e0 + Wd], w=[s_])
                    DMA("sp", gt[:, :, 0:Wd], GT_v[:, :, e0:e0 + Wd], w=[gt])
                    DMA("sp", x[:, :, 0:Wd], xT_v[:, :, OWN0 + e0:OWN0 + e0 + Wd], w=[x])
                    for m in range(8):
                        ab = pr.next()
                        for c in range(8):
                            MM(ab[:, 0:Wd], Wap[:, c, m * 128:(m + 1) * 128], o[:, c, 0:Wd], c == 0, c == 7, r=[Wap, o], w=[ab])
                        sbk = pr.next()
                        for c in range(4):
                            MM(sbk[:, 0:Wd], Wsp[:, c, m * 128:(m + 1) * 128], s_[:, c, 0:Wd], c == 0, c == 3, r=[Wsp, s_], w=[sbk])
                        t1, t2 = t_r.next(), t_r.next()
                        TT("dve", t1[:, 0:Wd], ab[:, 0:Wd], gt[:, m, 0:Wd], ALU.mult, r=[ab, gt], w=[t1])
                        TT("dve", t2[:, 0:Wd], sbk[:, 0:Wd], gt[:, 8 + m, 0:Wd], ALU.mult, r=[sbk, gt], w=[t2])
                        TT("pool", mix[:, m, 0:Wd], t1[:, 0:Wd], t2[:, 0:Wd], ALU.add, r=[t1, t2], w=[mix])
                    h_ = hh.next()
                    for m in range(8):
                        db = pr.next()
                        for c in range(8):
                            MM(db[:, 0:Wd], Wout[:, c, m * 128:(m + 1) * 128], mix[:, c, 0:Wd], c == 0, c == 7, r=[Wout, mix], w=[db])
                        TT("dve", h_[:, m, 0:Wd], x[:, m, 0:Wd], db[:, 0:Wd], ALU.add, r=[x, db], w=[h_])
                    ACT(sq[:, :, 0:Wd], h_[:, :, 0:Wd], AF.Square, r=[h_], w=[sq])
                    ssb = pr.next()
                    for c in range(8):
                        MM(ssb[:, 0:Wd], ones_bf[:], sq[:, c, 0:Wd], c == 0, c == 7, r=[ones_bf, sq], w=[ssb])
                    sd = sd_r.next()
                    ACT(sd[:, 0:Wd], ssb[:, 0:Wd], AF.Sqrt, r=[ssb], w=[sd], scale=1.0 / D, bias=epsb[:])
                    RECIP(sd[:, 0:Wd], sd[:, 0:Wd], r=[sd], w=[sd])
                    hn = hn_r.next()
                    for c in range(8):
                        STT("dve", hn[:, c, 0:Wd], h_[:, c, 0:Wd], g2[:, c:c + 1], sd[:, 0:Wd], ALU.mult, ALU.mult, r=[h_, g2, sd], w=[hn])
                    DMA("sp", HT_v[:, :, e0:e0 + Wd], h_[:, :, 0:Wd], r=[h_])
                    DMA("sp", HNT_v[:, :, e0:e0 + Wd], hn[:, :, 0:Wd], r=[hn])
                P.barrier()
                P.emit("ph4a")

        if stop_after >= 5:
            with ExitStack() as ps:
                TB = 256
                Wup = sb(ps, "Wup", [128, 8, 5632], BF)
                Wdn = sb(ps, "Wdn", [128, 22, 1024], BF)
                w_up_v = w_up.rearrange("(c p) n -> p c n", p=128)
                for c in range(8):
                    DMA("pool", Wup[:, c, :], w_up_v[:, c, :], w=[Wup])
                DMA("pool", Wdn[:], w_down.rearrange("(j p) n -> p j n", p=128), w=[Wdn])
                convw = sb(ps, "convw", [128, 44, 3], F32)
                convb = sb(ps, "convb", [128, 44], F32)
                carry = sb(ps, "carry", [128, 44, 2], F32)
                DMA("sp", convw[:], convw_d[:, :, :], w=[convw])
                DMA("sp", convb[:], convb_d[:, :], w=[convb])
                P.op("dve", lambda e: e.memset(carry[:], 0.0), w=[carry])
                hn_r = Ring([sb(ps, f"hnb{i}", [128, 8, TB], BF) for i in range(2)])
                hr_r = Ring([sb(ps, f"hrb{i}", [128, 8, TB], F32) for i in range(2)])
                ub_r = Ring([sb(ps, f"ubf{i}", [128, 2 + TB], F32) for i in range(4)])
                a_r = Ring([sb(ps, f"aa{i}", [128, TB], F32) for i in range(4)])
                sg_r = Ring([sb(ps, f"sg{i}", [128, TB], F32) for i in range(2)])
                acts = sb(ps, "acts", [128, 22, TB], BF)
                out_r = Ring([sb(ps, f"ob{i}", [128, 8, TB], F32) for i in range(1)])
                pr = Ring(banks[0:4])
                dr = Ring(banks[4:8])
                tiles = [(0, 128, True)] + [(128 + TB * i, TB, False) for i in range(4096 // TB)]
                for (e0, Wd, halo) in tiles:
                    hn = hn_r.next()
                    DMA("sp", hn[:, :, 0:Wd], HNT_v[:, :, e0:e0 + Wd], w=[hn])
                    if not halo:
                        hres = hr_r.next()
                        DMA("sp", hres[:, :, 0:Wd], HT_v[:, :, e0:e0 + Wd], w=[hres])
                    for j in range(22):
                        av = []
                        for mt in (j, 22 + j):
                            ub = pr.next()
                            for c in range(8):
                                MM(ub[:, 0:Wd], Wup[:, c, mt * 128:(mt + 1) * 128], hn[:, c, 0:Wd], c == 0, c == 7, r=[Wup, hn], w=[ub])
                            if halo:
                                ACT(carry[:, mt, :], ub[:, Wd - 2:Wd], AF.Copy, r=[ub], w=[carry])
                                continue
                            uf = ub_r.next()
                            COPY("pool", uf[:, 0:2], carry[:, mt, :], r=[carry], w=[uf])
                            ACT(uf[:, 2:2 + Wd], ub[:, 0:Wd], AF.Copy, r=[ub], w=[uf])
                            COPY("pool", carry[:, mt, :], uf[:, Wd:Wd + 2], r=[uf], w=[carry])
                            a = a_r.next()
                            TS("dve", a[:, 0:Wd], uf[:, 2:2 + Wd], convw[:, mt, 2:3], convb[:, mt:mt + 1], ALU.mult, ALU.add, r=[uf, convw, convb], w=[a])
                            STT("dve", a[:, 0:Wd], uf[:, 1:1 + Wd], convw[:, mt, 1:2], a[:, 0:Wd], ALU.mult, ALU.add, r=[uf, convw, a], w=[a])
                            STT("dve", a[:, 0:Wd], uf[:, 0:Wd], convw[:, mt, 0:1], a[:, 0:Wd], ALU.mult, ALU.add, r=[uf, convw, a], w=[a])
                            av.append(a)
                        if halo:
                            continue
                        sg = sg_r.next()
                        ACT(sg[:, 0:Wd], av[0][:, 0:Wd], AF.Silu, r=[av[0]], w=[sg])
                        TT("pool", acts[:, j, 0:Wd], sg[:, 0:Wd], av[1][:, 0:Wd], ALU.mult, r=[sg, av[1]], w=[acts])
                    if halo:
                        continue
                    ob = out_r.next()
                    for m in range(8):
                        db = dr.next()
                        for j in range(22):
                            MM(db[:, 0:Wd], Wdn[:, j, m * 128:(m + 1) * 128], acts[:, j, 0:Wd], j == 0, j == 21, r=[Wdn, acts], w=[db])
                        TT("dve", ob[:, m, 0:Wd], hres[:, m, 0:Wd], db[:, 0:Wd], ALU.add, r=[hres, db], w=[ob])
                    DMA("sp", yT_v[:, :, e0 - 128:e0 - 128 + Wd], ob[:, :, 0:Wd], r=[ob])
                P.barrier()
                P.emit("ph4b")

        P.barrier()
        P.emit("fin")
    return nc


def _rep(v, n=128):
    return np.ascontiguousarray(np.broadcast_to(np.asarray(v, np.float32)[None], (n,) + np.asarray(v).shape))


def prep_inputs(inp):
    f = np.float32
    x = np.asarray(inp["x"], f)
    common = {}
    common["ident"] = np.eye(128, dtype=f)
    perm = np.zeros((128, 128), f)
    for dp in range(128):
        blk, d = divmod(dp, 64)
        perm[blk * 64 + (d + 32) % 64, dp] = 1.0
    common["perm"] = perm
    common["tri"] = np.triu(np.ones((128, 128), f))
    common["jrow"] = _rep(np.arange(1, T + 1, dtype=f))
    common["g1"] = np.ascontiguousarray(np.asarray(inp["norm1_gain"], f)[0].reshape(8, 128).T)
    common["g2"] = np.ascontiguousarray(np.asarray(inp["norm2_gain"], f)[0].reshape(8, 128).T)
    gq = np.tile(np.asarray(inp["q_norm_gain"], f)[0], 2)
    gk = np.tile(np.asarray(inp["k_norm_gain"], f)[0], 2)
    common["gqk"] = np.ascontiguousarray(np.stack([gq, gk], axis=1))
    lam4 = np.stack([np.asarray(inp[k], f)[0] for k in ("lambda_q1", "lambda_k1", "lambda_q2", "lambda_k2")])
    common["lamp"] = _rep(lam4)
    common["subg"] = _rep(np.asarray(inp["subln_gain"], f)[0])
    common["w_in"] = np.ascontiguousarray(np.asarray(inp["w_in"], f)[0])
    common["w_ap"] = np.ascontiguousarray(np.asarray(inp["w_attn_proj"], f)[0])
    common["w_glu"] = np.ascontiguousarray(np.asarray(inp["w_glu"], f)[0])
    common["bglu"] = np.ascontiguousarray(np.asarray(inp["b_glu"], f)[0].reshape(4, 128).T)
    common["w_sp"] = np.ascontiguousarray(np.asarray(inp["w_ssm_proj"], f)[0])
    common["w_out"] = np.ascontiguousarray(np.asarray(inp["w_out"], f)[0])
    common["w_up"] = np.ascontiguousarray(np.asarray(inp["w_up"], f)[0])
    cw = np.asarray(inp["conv_w"], f)[0]
    common["convw"] = np.ascontiguousarray(cw.reshape(3, 44, 128).transpose(2, 1, 0))
    common["convb"] = np.ascontiguousarray(np.asarray(inp["conv_b"], f)[0].reshape(44, 128).T)
    common["w_down"] = np.ascontiguousarray(np.asarray(inp["w_down"], f)[0])

    def st(a):
        a = np.asarray(a, f)
        a = a.reshape((16, 2, 64) + a.shape[2:])
        a = np.moveaxis(a, 0, 2)
        return np.ascontiguousarray(a.reshape((128, 16) + a.shape[3:]))
    a_re = np.asarray(inp["ssm_a_re"], f)[0]
    a_im = np.asarray(inp["ssm_a_im"], f)[0]
    ldt = np.broadcast_to(np.asarray(inp["ssm_log_dt"], f)[0][:, None], (32, 64))
    common["s5a"] = np.ascontiguousarray(np.stack([st(a_re), st(a_im), st(ldt)], axis=1))
    common["s5b"] = np.ascontiguousarray(np.stack([st(np.asarray(inp["ssm_b_re"], f)[0]), st(np.asarray(inp["ssm_b_im"], f)[0])], axis=1))
    cre = np.asarray(inp["ssm_c_re"], f)[0].transpose(0, 2, 1)
    cim = np.asarray(inp["ssm_c_im"], f)[0].transpose(0, 2, 1)
    common["s5c"] = np.ascontiguousarray(np.stack([st(cre), st(cim)], axis=1))
    common["s5d"] = np.ascontiguousarray(np.asarray(inp["ssm_d"], f)[0].reshape(4, 128).T)

    inv_freq = (1.0 / (10000.0 ** (np.arange(0, 64, 2, dtype=f) / f(64)))).astype(f)
    maps = []
    for core in range(8):
        b, r = divmod(core, 4)
        w0 = 4096 * (r - 3)
        pos = np.arange(w0, w0 + WIN)
        vmask = pos >= 0
        xw = np.zeros((WIN, D), f)
        xw[vmask] = x[b, pos[vmask]]
        ang = (np.where(vmask, pos, 0).astype(f)[:, None] * inv_freq[None, :]).astype(f)
        cos = np.cos(ang).astype(f).T
        sin = np.sin(ang).astype(f).T
        m = dict(common)
        m["xT"] = np.ascontiguousarray(xw.T)
        m["cosT"] = np.ascontiguousarray(np.concatenate([cos, cos, cos, cos], axis=0))
        m["sinT"] = np.ascontiguousarray(np.concatenate([-sin, sin, -sin, sin], axis=0))
        m["validT"] = np.ascontiguousarray(vmask.astype(f).reshape(128, 128).T)
        maps.append(m)
    return maps


_NC_CACHE = {}


def kernel(**inputs):
    maps = prep_inputs(inputs)
    if "nc" not in _NC_CACHE:
        _NC_CACHE["nc"] = build_nc()
    res = run_bass_kernel_spmd(_NC_CACHE["nc"], maps, core_ids=list(range(8)))
    out = np.zeros((2, 16384, D), np.float32)
    for core in range(8):
        b, r = divmod(core, 4)
        out[b, 4096 * r:4096 * (r + 1), :] = np.asarray(res.results[core]["yT"], np.float32).T
    return out
```

```python
import math
from contextlib import ExitStack
import numpy as np
import concourse.bass as bass
import concourse.mybir as mybir
from concourse.bass_utils import run_bass_kernel_spmd

F32 = mybir.dt.float32
BF = mybir.dt.bfloat16
AF = mybir.ActivationFunctionType
ALU = mybir.AluOpType
AX = mybir.AxisListType

D = 1024
WIN = 16384
OWN0 = 12160
NE = WIN - OWN0
T = 512
NT = WIN // T
EPS = 1e-6
LAM_INIT = 0.2
ENGS = ("sp", "act", "dve", "pool", "pe")
LIMIT = 24000
NDMA = 24


class Buf:
    __slots__ = ("t", "lw", "rd")

    def __init__(self, t):
        self.t = t
        self.lw = None
        self.rd = {}

    def __getitem__(self, k):
        return self.t[k]


class Prog:
    def __init__(self, nc, stack):
        self.nc = nc
        self.stack = stack
        self.ops = {e: [] for e in ENGS}
        self.sem = {}
        self.cnt = {}
        self.waited = {e: {} for e in ENGS}
        self.nsem = 0
        self.pesems = set()
        for e in ENGS:
            self._rot(e)
        self.dsems = [self._new("d") for _ in range(NDMA)]
        self.dcnt = [0] * NDMA
        self.di = 0
        self.nops = 0

    def _new(self, pfx):
        self.nsem += 1
        return self.stack.enter_context(self.nc.semaphore(f"{pfx}{self.nsem}"))

    def _rot(self, e):
        self.sem[e] = self._new(e)
        self.cnt[e] = 0
        if e == "pe":
            self.pesems.add(id(self.sem[e]))

    def op(self, eng, fn, r=(), w=(), dma=False):
        deps = []
        for b in r:
            if b.lw is not None:
                deps.append(b.lw)
        for b in w:
            if b.lw is not None:
                deps.append(b.lw)
            deps.extend(b.rd.values())
        if dma:
            i = self.di
            self.di = (self.di + 1) % NDMA
            sem = self.dsems[i]
            if self.dcnt[i] > 0:
                deps.append((sem, self.dcnt[i]))
            self.dcnt[i] += 16
            tok = (sem, self.dcnt[i])
            inc = 16
        else:
            if self.cnt[eng] >= LIMIT:
                self._rot(eng)
            self.cnt[eng] += 1
            tok = (self.sem[eng], self.cnt[eng])
            inc = 1
        m = {}
        for sm, v in deps:
            if eng == "pe" and id(sm) in self.pesems:
                continue
            k = id(sm)
            if k not in m or m[k][1] < v:
                m[k] = (sm, v)
        wd = self.waited[eng]
        waits = []
        for k, (sm, v) in m.items():
            if wd.get(k, 0) < v:
                waits.append((sm, v))
                wd[k] = v
        self.ops[eng].append((fn, waits, tok[0], inc))
        for b in r:
            k = id(tok[0])
            b.rd[k] = tok
        for b in w:
            b.lw = tok
            b.rd = {}
        self.nops += 1
        return tok

    def barrier(self):
        toks = []
        for e in ENGS:
            if self.cnt[e] > 0:
                toks.append((self.sem[e], self.cnt[e]))
        for i in range(NDMA):
            if self.dcnt[i] > 0:
                toks.append((self.dsems[i], self.dcnt[i]))
        for e in ENGS:
            wd = self.waited[e]
            waits = []
            for sm, v in toks:
                if sm is self.sem[e]:
                    continue
                if wd.get(id(sm), 0) < v:
                    waits.append((sm, v))
                    wd[id(sm)] = v
            self.ops[e].append((None, waits, None, 0))

    def emit(self, name):
        nc = self.nc
        with nc.Block(name) as blk:
            for eng, dec in (("sp", blk.sync), ("act", blk.scalar), ("dve", blk.vector),
                             ("pool", blk.gpsimd), ("pe", blk.tensor)):
                ops = self.ops[eng]

                def body(e, ops=ops):
                    for fn, waits, sem, inc in ops:
                        for sm, v in waits:
                            e.wait_ge(sm, v)
                        if fn is not None:
                            fn(e).then_inc(sem, inc)
                dec(body)
        self.ops = {e: [] for e in ENGS}


class Ring:
    def __init__(self, bufs):
        self.b = bufs
        self.i = 0

    def next(self):
        b = self.b[self.i]
        self.i = (self.i + 1) % len(self.b)
        return b


def build_nc(stop_after=5, dbg=False):
    nc = bass.Bass("TRN2", target_bir_lowering=False)

    def din(name, shape, dt=F32):
        return nc.dram_tensor(name, list(shape), dt, kind="ExternalInput").ap()

    def dscr(name, shape, dt):
        return nc.dram_tensor(name, list(shape), dt, kind=("ExternalOutput" if (dbg and name in ("ST", "OT", "HT", "HNT")) else "Internal")).ap()

    xT = din("xT", [D, WIN])
    cosT = din("cosT", [128, WIN])
    sinT = din("sinT", [128, WIN])
    validT = din("validT", [128, 128])
    ident_d = din("ident", [128, 128])
    perm_d = din("perm", [128, 128])
    tri_d = din("tri", [128, 128])
    jrow_d = din("jrow", [128, T])
    g1_d = din("g1", [128, 8])
    g2_d = din("g2", [128, 8])
    gqk_d = din("gqk", [128, 2])
    lamp_d = din("lamp", [128, 4, 64])
    subg_d = din("subg", [128, 128])
    w_in = din("w_in", [D, 5632])
    w_ap = din("w_ap", [D, D])
    w_glu = din("w_glu", [512, 512])
    bglu_d = din("bglu", [128, 4])
    w_sp = din("w_sp", [512, D])
    w_out = din("w_out", [D, D])
    w_up = din("w_up", [D, 5632])
    convw_d = din("convw", [128, 44, 3])
    convb_d = din("convb", [128, 44])
    w_down = din("w_down", [2816, D])
    s5a_d = din("s5a", [128, 3, 16])
    s5b_d = din("s5b", [128, 2, 16, 16])
    s5c_d = din("s5c", [128, 2, 16, 16])
    s5d_d = din("s5d", [128, 4])
    yT = nc.dram_tensor("yT", [D, 4096], F32, kind="ExternalOutput").ap()

    KT = dscr("KT", [8, 128, WIN], BF)
    VS = dscr("VS", [128, 128, 8 * 130], BF)
    QT = dscr("QT", [8, 128, NE], BF)
    UT = dscr("UT", [512, WIN], BF)
    GT = dscr("GT", [2048, NE], BF)
    ST = dscr("ST", [512, NE], BF)
    OT = dscr("OT", [D, NE], BF)
    HT = dscr("HT", [D, NE], F32)
    HNT = dscr("HNT", [D, NE], BF)
    dbg_out = {}

    with ExitStack() as gs:
        P = Prog(nc, gs)

        _cnt = [0]

        def sb(stack, name, shape, dt):
            _cnt[0] += 1
            return Buf(stack.enter_context(nc.sbuf_tensor(f"sb{_cnt[0]}_{name}", list(shape), dt)))

        banks = [Buf(gs.enter_context(nc.psum_tensor(f"pb{i}", [128, 512], F32))) for i in range(8)]

        ones_bf = sb(gs, "ones_bf", [128, 128], BF)
        blk_bf = sb(gs, "blk_bf", [128, 128], BF)
        ident_f = sb(gs, "ident_f", [128, 128], F32)
        ident_b = sb(gs, "ident_b", [128, 128], BF)
        perm_b = sb(gs, "perm_b", [128, 128], BF)
        tri_b = sb(gs, "tri_b", [128, 128], BF)
        g1 = sb(gs, "g1", [128, 8], F32)
        g2 = sb(gs, "g2", [128, 8], F32)
        gqk = sb(gs, "gqk", [128, 2], F32)
        neglam = sb(gs, "neglam", [128, 1], F32)
        subg = sb(gs, "subg", [128, 128], F32)
        valid = sb(gs, "valid", [128, 128], F32)
        epsb = sb(gs, "epsb", [128, 1], F32)

        def dma(eng, out_b, out_ap, in_b, in_ap):
            r = [in_b] if in_b is not None else []
            w = [out_b] if out_b is not None else []
            return P.op(eng, lambda e: e.dma_start(out=out_ap, in_=in_ap), r=r, w=w, dma=True)


        def ACT(out, in_, func, r, w, **kw):
            return P.op("act", lambda e: e.activation(out=out, in_=in_, func=func, **kw), r=r, w=w)

        def MM(out, lhsT, rhs, start, stop, r, w):
            return P.op("pe", lambda e: e.matmul(out, lhsT=lhsT, rhs=rhs, start=start, stop=stop), r=r, w=w)

        def TT(eng, out, in0, in1, op, r, w):
            return P.op(eng, lambda e: e.tensor_tensor(out=out, in0=in0, in1=in1, op=op), r=r, w=w)

        def STT(eng, out, in0, scalar, in1, op0, op1, r, w):
            return P.op(eng, lambda e: e.scalar_tensor_tensor(out=out, in0=in0, scalar=scalar, in1=in1, op0=op0, op1=op1), r=r, w=w)

        def TS(eng, out, in0, s1, s2, op0, op1, r, w):
            if s2 is None:
                return P.op(eng, lambda e: e.tensor_scalar(out=out, in0=in0, scalar1=s1, scalar2=None, op0=op0), r=r, w=w)
            return P.op(eng, lambda e: e.tensor_scalar(out=out, in0=in0, scalar1=s1, scalar2=s2, op0=op0, op1=op1), r=r, w=w)

        def RECIP(out, in_, r, w):
            return P.op("dve", lambda e: e.reciprocal(out=out, in_=in_), r=r, w=w)

        def COPY(eng, out, in_, r, w):
            return P.op(eng, lambda e: e.tensor_copy(out=out, in_=in_), r=r, w=w)

        def DMA(eng, out_ap, in_ap, r=(), w=()):
            return P.op(eng, lambda e: e.dma_start(out=out_ap, in_=in_ap), r=list(r), w=list(w), dma=True)

        with ExitStack() as ps:
            lamp = sb(ps, "lamp", [128, 4, 64], F32)
            lt = sb(ps, "lt", [128, 2, 64], F32)
            ls = sb(ps, "ls", [128, 2], F32)
            P.op("dve", lambda e: e.memset(ones_bf[:], 1.0), w=[ones_bf])
            P.op("dve", lambda e: e.memset(epsb[:], EPS), w=[epsb])
            P.op("dve", lambda e: e.memset(blk_bf[:], 0.0), w=[blk_bf])
            P.op("dve", lambda e: e.memset(blk_bf[0:64, 0:64], 1.0), w=[blk_bf])
            P.op("dve", lambda e: e.memset(blk_bf[64:128, 64:128], 1.0), w=[blk_bf])
            dma("sp", ident_f, ident_f[:], None, ident_d[:, :])
            dma("pool", ident_b, ident_b[:], None, ident_d[:, :])
            dma("pool", perm_b, perm_b[:], None, perm_d[:, :])
            dma("pool", tri_b, tri_b[:], None, tri_d[:, :])
            dma("sp", g1, g1[:], None, g1_d[:, :])
            dma("sp", g2, g2[:], None, g2_d[:, :])
            dma("sp", gqk, gqk[:], None, gqk_d[:, :])
            dma("sp", subg, subg[:], None, subg_d[:, :])
            dma("sp", valid, valid[:], None, validT[:, :])
            dma("sp", lamp, lamp[:], None, lamp_d[:, :, :])
            P.op("dve", lambda e: e.tensor_tensor(out=lt[:, 0, :], in0=lamp[:, 0, :], in1=lamp[:, 1, :], op=ALU.mult), r=[lamp], w=[lt])
            P.op("dve", lambda e: e.tensor_tensor(out=lt[:, 1, :], in0=lamp[:, 2, :], in1=lamp[:, 3, :], op=ALU.mult), r=[lamp], w=[lt])
            P.op("dve", lambda e: e.tensor_reduce(out=ls[:], in_=lt[:], axis=AX.X, op=ALU.add), r=[lt], w=[ls])
            P.op("act", lambda e: e.activation(out=ls[:], in_=ls[:], func=AF.Exp), r=[ls], w=[ls])
            P.op("dve", lambda e: e.tensor_tensor(out=neglam[:], in0=ls[:, 1:2], in1=ls[:, 0:1], op=ALU.subtract), r=[ls], w=[neglam])
            P.op("dve", lambda e: e.tensor_scalar(out=neglam[:], in0=neglam[:], scalar1=-LAM_INIT, scalar2=None, op0=ALU.add), r=[neglam], w=[neglam])
            P.op("dve", lambda e: e.tensor_scalar(out=gqk[:, 0:1], in0=gqk[:, 0:1], scalar1=0.125, scalar2=None, op0=ALU.mult), r=[gqk], w=[gqk])
            P.barrier()
            P.emit("ph0")

        if stop_after >= 1:
            with ExitStack() as ps:
                Wq = sb(ps, "Wq", [128, 8, 1024], BF)
                Wk = sb(ps, "Wk", [128, 8, 1024], BF)
                Wv = sb(ps, "Wv", [128, 8, 1024], BF)
                Wu = sb(ps, "Wu", [128, 8, 512], BF)
                Wg = sb(ps, "Wg", [128, 8, 2048], BF)
                w_in_v = w_in.rearrange("(c p) n -> p c n", p=128)
                for c in range(8):
                    DMA("pool", Wk[:, c, :], w_in_v[:, c, 1024:2048], w=[Wk])
                    DMA("pool", Wv[:, c, :], w_in_v[:, c, 2048:3072], w=[Wv])
                    DMA("pool", Wu[:, c, :], w_in_v[:, c, 3072:3584], w=[Wu])
                for c in range(8):
                    DMA("pool", Wq[:, c, :], w_in_v[:, c, 0:1024], w=[Wq])
                    DMA("pool", Wg[:, c, :], w_in_v[:, c, 3584:5632], w=[Wg])
                xr = Ring([sb(ps, f"x{i}", [128, 8, T], F32) for i in range(2)])
                csr = Ring([sb(ps, f"cs{i}", [128, 2, T], F32) for i in range(2)])
                sq = sb(ps, "sq", [128, 8, T], BF)
                xnr = Ring([sb(ps, f"xn{i}", [128, 8, T], BF) for i in range(2)])
                sdr = Ring([sb(ps, f"sd{i}", [128, T], F32) for i in range(3)])
                sqh = Ring([sb(ps, f"sqh{i}", [128, T], BF) for i in range(2)])
                kgr = Ring([sb(ps, f"kg{i}", [128, T], BF) for i in range(2)])
                t1r = Ring([sb(ps, f"t1{i}", [128, T], F32) for i in range(2)])
                t2r = Ring([sb(ps, f"t2{i}", [128, T], F32) for i in range(2)])
                kor = Ring([sb(ps, f"ko{i}", [128, T], BF) for i in range(3)])
                vor = Ring([sb(ps, f"vo{i}", [128, 8, 130], BF) for i in range(3)])
                uor = Ring([sb(ps, f"uo{i}", [128, T], BF) for i in range(3)])
                xT_v = xT.rearrange("(c p) t -> p c t", p=128)
                UT_v = UT.rearrange("(q p) t -> p q t", p=128)
                GT_v = GT.rearrange("(m p) t -> p m t", p=128)
                pr = Ring(banks)

                def qk_post(ps_b, gcol, cs, c0, c1, dst_ap):
                    n = c1 - c0
                    s_ = sqh.next()
                    ACT(s_[:, 0:n], ps_b[:, c0:c1], AF.Square, r=[ps_b], w=[s_])
                    ssb = pr.next()
                    MM(ssb[:, 0:n], blk_bf[:], s_[:, 0:n], True, True, r=[blk_bf, s_], w=[ssb])
                    sd = sdr.next()
                    ACT(sd[:, 0:n], ssb[:, 0:n], AF.Sqrt, r=[ssb], w=[sd], scale=1.0 / 64.0, bias=epsb[:])
                    RECIP(sd[:, 0:n], sd[:, 0:n], r=[sd], w=[sd])
                    kg = kgr.next()
                    ACT(kg[:, 0:n], ps_b[:, c0:c1], AF.Copy, r=[ps_b, gqk], w=[kg], scale=gqk[:, gcol:gcol + 1])
                    swb = pr.next()
                    MM(swb[:, 0:n], perm_b[:], kg[:, 0:n], True, True, r=[perm_b, kg], w=[swb])
                    t1 = t1r.next()
                    STT("dve", t1[:, 0:n], ps_b[:, c0:c1], gqk[:, gcol:gcol + 1], cs[:, 0, c0:c1], ALU.mult, ALU.mult, r=[ps_b, gqk, cs], w=[t1])
                    t2 = t2r.next()
                    TT("dve", t2[:, 0:n], swb[:, 0:n], cs[:, 1, c0:c1], ALU.mult, r=[swb, cs], w=[t2])
                    TT("pool", t1[:, 0:n], t1[:, 0:n], t2[:, 0:n], ALU.add, r=[t1, t2], w=[t1])
                    ko = kor.next()
                    TT("dve", ko[:, 0:n], t1[:, 0:n], sd[:, 0:n], ALU.mult, r=[t1, sd], w=[ko])
                    DMA("sp", dst_ap, ko[:, 0:n], r=[ko])

                SECT = 255
                for tt in range(NT):
                    t0 = tt * T
                    x = xr.next()
                    for c in range(0, 8, 4):
                        DMA("sp", x[:, c:c + 4, :], xT_v[:, c:c + 4, t0:t0 + T], w=[x])
                    cs = csr.next()
                    DMA("sp", cs[:, 0, :], cosT[:, t0:t0 + T], w=[cs])
                    DMA("sp", cs[:, 1, :], sinT[:, t0:t0 + T], w=[cs])
                    ACT(sq[:], x[:], AF.Square, r=[x], w=[sq])
                    ssb = pr.next()
                    for c in range(8):
                        MM(ssb[:], ones_bf[:], sq[:, c, :], c == 0, c == 7, r=[ones_bf, sq], w=[ssb])
                    sd = sdr.next()
                    ACT(sd[:], ssb[:], AF.Sqrt, r=[ssb], w=[sd], scale=1.0 / D, bias=epsb[:])
                    RECIP(sd[:], sd[:], r=[sd], w=[sd])
                    xn = xnr.next()
                    for c in range(8):
                        STT("dve", xn[:, c, :], x[:, c, :], g1[:, c:c + 1], sd[:], ALU.mult, ALU.mult, r=[x, g1, sd], w=[xn])
                    for h in range(8 if SECT & 2 else 0):
                        kb = pr.next()
                        for c in range(8):
                            MM(kb[:], Wk[:, c, h * 128:(h + 1) * 128], xn[:, c, :], c == 0, c == 7, r=[Wk, xn], w=[kb])
                        qk_post(kb, 1, cs, 0, T, KT[h, :, t0:t0 + T])
                    for j in range(4 if SECT & 4 else 0):
                        vo = vor.next()
                        kt = tt * 4 + j
                        for half in range(2):
                            vb = pr.next()
                            for c in range(8):
                                MM(vb[:], xn[:, c, j * 128:(j + 1) * 128], Wv[:, c, half * 512:(half + 1) * 512], c == 0, c == 7, r=[Wv, xn], w=[vb])
                            ACT(vo[:, half * 4:half * 4 + 4, 0:128], vb[:].rearrange("p (h e) -> p h e", h=4), AF.Copy, r=[vb], w=[vo])
                        COPY("dve", vo[:, :, 128:130], valid[:, kt:kt + 1].to_broadcast([128, 8, 2]), r=[valid], w=[vo])
                        DMA("sp", VS[kt, :, :], vo[:].rearrange("p h e -> p (h e)"), r=[vo])
                    for q in range(4 if SECT & 8 else 0):
                        ub = pr.next()
                        for c in range(8):
                            MM(ub[:], Wu[:, c, q * 128:(q + 1) * 128], xn[:, c, :], c == 0, c == 7, r=[Wu, xn], w=[ub])
                        uo = uor.next()
                        ACT(uo[:], ub[:], AF.Copy, r=[ub], w=[uo])
                        DMA("sp", UT_v[:, q, t0:t0 + T], uo[:], r=[uo])
                    if t0 + T > OWN0:
                        c0 = max(OWN0 - t0, 0)
                        n = T - c0
                        e0 = t0 + c0 - OWN0
                        for h in range(8):
                            qb = pr.next()
                            for c in range(8):
                                MM(qb[:, c0:T], Wq[:, c, h * 128:(h + 1) * 128], xn[:, c, c0:T], c == 0, c == 7, r=[Wq, xn], w=[qb])
                            qk_post(qb, 0, cs, c0, T, QT[h, :, e0:e0 + n])
                        for m in range(16):
                            gb = pr.next()
                            for c in range(8):
                                MM(gb[:, c0:T], Wg[:, c, m * 128:(m + 1) * 128], xn[:, c, c0:T], c == 0, c == 7, r=[Wg, xn], w=[gb])
                            uo = uor.next()
                            ACT(uo[:, 0:n], gb[:, c0:T], AF.Sigmoid, r=[gb], w=[uo])
                            DMA("sp", GT_v[:, m, e0:e0 + n], uo[:, 0:n], r=[uo])
                P.barrier()
                P.emit("ph1")
        if stop_after >= 2:
            with ExitStack() as ps:
                s5a = sb(ps, "s5a", [128, 3, 16], F32)
                s5b = sb(ps, "s5b", [128, 2, 16, 16], F32)
                s5c = sb(ps, "s5c", [128, 2, 16, 16], F32)
                s5d = sb(ps, "s5d", [128, 4], F32)
                bglu = sb(ps, "bglu", [128, 4], F32)
                jrow = sb(ps, "jrow", [128, T], F32)
                twopi = sb(ps, "twopi", [128, T], F32)
                negpi = sb(ps, "negpi", [128, 1], F32)
                sm = [sb(ps, f"sm{i}", [128, 16], F32) for i in range(12)]
                cosJ = sb(ps, "cosJ", [128, 16, T], F32)
                sinJ = sb(ps, "sinJ", [128, 16, T], F32)
                bb = sb(ps, "bb", [128, 2, 16, 16], F32)
                bd = sb(ps, "bd", [128, 16, 128], F32)
                Bf = [sb(ps, f"Bf{i}", [128, 16, 128], BF) for i in range(2)]
                Cf = [sb(ps, f"Cf{i}", [128, 16, 128], BF) for i in range(2)]
                Wgl = sb(ps, "Wgl", [128, 4, 512], BF)
                w0 = [sb(ps, f"w0{i}", [128, 16], F32) for i in range(2)]
                tmp4 = sb(ps, "tmp4", [128, 4], F32)
                DMA("sp", s5a[:], s5a_d[:, :, :], w=[s5a])
                DMA("sp", s5b[:], s5b_d[:, :, :, :], w=[s5b])
                DMA("sp", s5c[:], s5c_d[:, :, :, :], w=[s5c])
                DMA("sp", s5d[:], s5d_d[:, :], w=[s5d])
                DMA("sp", bglu[:], bglu_d[:, :], w=[bglu])
                DMA("sp", jrow[:], jrow_d[:, :], w=[jrow])
                DMA("pool", Wgl[:], w_glu.rearrange("(q p) n -> p q n", p=128), w=[Wgl])
                P.op("dve", lambda e: e.memset(twopi[:], 2.0 * math.pi), w=[twopi])
                P.op("dve", lambda e: e.memset(negpi[:], -math.pi), w=[negpi])
                P.op("dve", lambda e: e.memset(w0[0][:], 0.0), w=[w0[0]])
                P.op("dve", lambda e: e.memset(w0[1][:], 0.0), w=[w0[1]])
                dt_, ardt, th, mag, lbre, lbim, den, cr, ci, ta, tb, tc = sm
                AR, AI, LDT = s5a[:, 0, :], s5a[:, 1, :], s5a[:, 2, :]
                ACT(dt_[:], LDT, AF.Exp, r=[s5a], w=[dt_])
                TT("dve", ardt[:], AR, dt_[:], ALU.mult, r=[s5a, dt_], w=[ardt])
                TT("dve", th[:], AI, dt_[:], ALU.mult, r=[s5a, dt_], w=[th])
                ACT(mag[:], ardt[:], AF.Exp, r=[ardt], w=[mag])
                rq = sb(ps, "rq", [128, T], F32)
                rm = sb(ps, "rm", [128, T], F32)
                ki = sb(ps, "ki", [128, T], mybir.dt.int32)
                for sp in range(16):
                    for tab, off in ((sinJ, 0.0), (cosJ, 0.5 * math.pi)):
                        xv = tab[:, sp, :]
                        TS("dve", xv, jrow[:], th[:, sp:sp + 1], off, ALU.mult, ALU.add, r=[jrow, th], w=[tab])
                        TS("dve", rq[:], xv, 1.0 / (2.0 * math.pi), None, ALU.mult, None, r=[tab], w=[rq])
                        COPY("dve", ki[:], rq[:], r=[rq], w=[ki])
                        COPY("dve", rq[:], ki[:], r=[ki], w=[rq])
                        STT("dve", xv, rq[:], -2.0 * math.pi, xv, ALU.mult, ALU.add, r=[rq, tab], w=[tab])
                        TS("dve", rm[:], xv, math.pi, -2.0 * math.pi, ALU.is_gt, ALU.mult, r=[tab], w=[rm])
                        TT("dve", xv, xv, rm[:], ALU.add, r=[tab, rm], w=[tab])
                        TS("dve", rm[:], xv, -math.pi, 2.0 * math.pi, ALU.is_lt, ALU.mult, r=[tab], w=[rm])
                        TT("dve", xv, xv, rm[:], ALU.add, r=[tab, rm], w=[tab])
                for tab in (sinJ, cosJ):
                    for half in range(2):
                        ACT(tab[:, half * 8:half * 8 + 8, :], tab[:, half * 8:half * 8 + 8, :], AF.Sin, r=[tab], w=[tab])
                cos1, sin1 = cosJ[:, :, 0], sinJ[:, :, 0]
                TT("dve", lbre[:], mag[:], cos1, ALU.mult, r=[mag, cosJ], w=[lbre])
                TT("dve", lbim[:], mag[:], sin1, ALU.mult, r=[mag, sinJ], w=[lbim])
                TS("dve", lbre[:], lbre[:], -1.0, None, ALU.add, None, r=[lbre], w=[lbre])
                TT("dve", den[:], AR, AR, ALU.mult, r=[s5a], w=[den])
                TT("dve", ta[:], AI, AI, ALU.mult, r=[s5a], w=[ta])
                TT("dve", den[:], den[:], ta[:], ALU.add, r=[den, ta], w=[den])
                RECIP(den[:], den[:], r=[den], w=[den])
                TT("dve", ta[:], lbre[:], AR, ALU.mult, r=[lbre, s5a], w=[ta])
                TT("dve", tb[:], lbim[:], AI, ALU.mult, r=[lbim, s5a], w=[tb])
                TT("dve", ta[:], ta[:], tb[:], ALU.add, r=[ta, tb], w=[ta])
                TT("dve", cr[:], ta[:], den[:], ALU.mult, r=[ta, den], w=[cr])
                TT("dve", ta[:], lbim[:], AR, ALU.mult, r=[lbim, s5a], w=[ta])
                TT("dve", tb[:], lbre[:], AI, ALU.mult, r=[lbre, s5a], w=[tb])
                TT("dve", ta[:], ta[:], tb[:], ALU.subtract, r=[ta, tb], w=[ta])
                TT("dve", ci[:], ta[:], den[:], ALU.mult, r=[ta, den], w=[ci])
                crb = cr[:].unsqueeze(2).to_broadcast([128, 16, 16])
                cib = ci[:].unsqueeze(2).to_broadcast([128, 16, 16])
                t16a = sb(ps, "t16a", [128, 16, 16], F32)
                BRE, BIM = s5b[:, 0, :, :], s5b[:, 1, :, :]
                TT("dve", bb[:, 0, :, :], BRE, crb, ALU.mult, r=[s5b, cr], w=[bb])
                TT("dve", t16a[:], BIM, cib, ALU.mult, r=[s5b, ci], w=[t16a])
                TT("dve", bb[:, 0, :, :], bb[:, 0, :, :], t16a[:], ALU.subtract, r=[bb, t16a], w=[bb])
                TT("dve", bb[:, 1, :, :], BIM, crb, ALU.mult, r=[s5b, cr, bb], w=[bb])
                TT("dve", t16a[:], BRE, cib, ALU.mult, r=[s5b, ci, bb], w=[t16a])
                TT("dve", bb[:, 1, :, :], bb[:, 1, :, :], t16a[:], ALU.add, r=[bb, t16a], w=[bb])
                pr = Ring(banks[2:8])
                def place(dst, src3, neg=False):
                    P.op("dve", lambda e: e.memset(dst[:], 0.0), w=[dst])
                    for a in range(2):
                        for s_ in range(4):
                            o_ap = dst[64 * a:64 * a + 64, s_:16:4, 32 * s_ + 16 * a:32 * s_ + 16 * a + 16]
                            i_ap = src3[64 * a:64 * a + 64, s_:16:4, :]
                            if neg:
                                TS("dve", o_ap, i_ap, -1.0, None, ALU.mult, None, r=[s5c, bb], w=[dst])
                            else:
                                COPY("dve", o_ap, i_ap, r=[s5c, bb], w=[dst])
                for i in range(2):
                    place(bd, bb[:, i, :, :])
                    for sp in range(16):
                        tb_ = pr.next()
                        P.op("pe", lambda e, tb_=tb_, sp=sp: e.transpose(tb_[:, 0:128], bd[:, sp, :], ident_f[:]), r=[bd, ident_f], w=[tb_])
                        ACT(Bf[i][:, sp, :], tb_[:, 0:128], AF.Copy, r=[tb_], w=[Bf[i]])
                place(Cf[0], s5c[:, 0, :, :])
                place(Cf[1], s5c[:, 1, :, :], neg=True)

                ur = Ring([sb(ps, f"u{i}", [128, 4, T], BF) for i in range(2)])
                tr = Ring([sb(ps, f"tq{i}", [128, T], F32) for i in range(8)])
                hr = Ring([sb(ps, f"hh{i}", [128, T], F32) for i in range(4)])
                wr = Ring([sb(ps, f"ww{i}", [128, T], F32) for i in range(6)])
                xr2 = Ring([sb(ps, f"xx{i}", [128, T], BF) for i in range(4)])
                yvr = Ring([sb(ps, f"yv{i}", [128, T], F32) for i in range(2)])
                g4r = Ring([sb(ps, f"g4{i}", [128, 4, T], BF) for i in range(2)])
                sor = Ring([sb(ps, f"so{i}", [128, T], BF) for i in range(2)])
                tnr = Ring([sb(ps, f"tn{i}", [128, 4], F32) for i in range(4)])
                UT_v = UT.rearrange("(q p) t -> p q t", p=128)
                ST_v = ST.rearrange("(q p) t -> p q t", p=128)
                ybk = Ring(banks[0:2])
                for ck in range(NT):
                    t0 = ck * T
                    u = ur.next()
                    DMA("sp", u[:], UT_v[:, :, t0:t0 + T], w=[u])
                    own = t0 + T > OWN0
                    c0 = max(OWN0 - t0, 0)
                    n = T - c0
                    e0 = t0 + c0 - OWN0
                    g4 = g4r.next() if own else None
                    for q in range(4):
                        yb = ybk.next() if own else None
                        for s_ in range(4):
                            sp = 4 * q + s_
                            bre = pr.next()
                            MM(bre[:], Bf[0][:, sp, :], u[:, q, :], True, True, r=[Bf[0], u], w=[bre])
                            bim = pr.next()
                            MM(bim[:], Bf[1][:, sp, :], u[:, q, :], True, True, r=[Bf[1], u], w=[bim])
                            t1, t2, t3, t4 = tr.next(), tr.next(), tr.next(), tr.next()
                            TT("dve", t1[:], bre[:], cosJ[:, sp, :], ALU.mult, r=[bre, cosJ], w=[t1])
                            TT("dve", t2[:], bim[:], sinJ[:, sp, :], ALU.mult, r=[bim, sinJ], w=[t2])
                            TT("dve", t3[:], bim[:], cosJ[:, sp, :], ALU.mult, r=[bim, cosJ], w=[t3])
                            TT("dve", t4[:], bre[:], sinJ[:, sp, :], ALU.mult, r=[bre, sinJ], w=[t4])
                            hre, him = hr.next(), hr.next()
                            TT("pool", hre[:], t1[:], t2[:], ALU.add, r=[t1, t2], w=[hre])
                            TT("pool", him[:], t3[:], t4[:], ALU.subtract, r=[t3, t4], w=[him])
                            wre, wim = wr.next(), wr.next()
                            rho = mag[:, sp:sp + 1].to_broadcast([128, T])
                            P.op("dve", lambda e, wre=wre, hre=hre, sp=sp, rho=rho: e.tensor_tensor_scan(out=wre[:], data0=rho, data1=hre[:], initial=w0[0][:, sp:sp + 1], op0=ALU.mult, op1=ALU.add), r=[mag, hre, w0[0]], w=[wre])
                            P.op("dve", lambda e, wim=wim, him=him, sp=sp, rho=rho: e.tensor_tensor_scan(out=wim[:], data0=rho, data1=him[:], initial=w0[1][:, sp:sp + 1], op0=ALU.mult, op1=ALU.add), r=[mag, him, w0[1]], w=[wim])
                            tn = tnr.next()
                            L = T - 1
                            TT("dve", tn[:, 0:1], wre[:, L:T], cosJ[:, sp, L:T], ALU.mult, r=[wre, cosJ], w=[tn])
                            TT("dve", tn[:, 1:2], wim[:, L:T], sinJ[:, sp, L:T], ALU.mult, r=[wim, sinJ], w=[tn])
                            TT("dve", tn[:, 2:3], wre[:, L:T], sinJ[:, sp, L:T], ALU.mult, r=[wre, sinJ], w=[tn])
                            TT("dve", tn[:, 3:4], wim[:, L:T], cosJ[:, sp, L:T], ALU.mult, r=[wim, cosJ], w=[tn])
                            TT("dve", w0[0][:, sp:sp + 1], tn[:, 0:1], tn[:, 1:2], ALU.subtract, r=[tn], w=[w0[0]])
                            TT("dve", w0[1][:, sp:sp + 1], tn[:, 2:3], tn[:, 3:4], ALU.add, r=[tn], w=[w0[1]])
                            if own:
                                a1, a2, a3, a4 = tr.next(), tr.next(), tr.next(), tr.next()
                                TT("pool", a1[:, c0:T], wre[:, c0:T], cosJ[:, sp, c0:T], ALU.mult, r=[wre, cosJ], w=[a1])
                                TT("pool", a2[:, c0:T], wim[:, c0:T], sinJ[:, sp, c0:T], ALU.mult, r=[wim, sinJ], w=[a2])
                                TT("pool", a3[:, c0:T], wre[:, c0:T], sinJ[:, sp, c0:T], ALU.mult, r=[wre, sinJ], w=[a3])
                                TT("pool", a4[:, c0:T], wim[:, c0:T], cosJ[:, sp, c0:T], ALU.mult, r=[wim, cosJ], w=[a4])
                                xre, xim = xr2.next(), xr2.next()
                                TT("pool", xre[:, c0:T], a1[:, c0:T], a2[:, c0:T], ALU.subtract, r=[a1, a2], w=[xre])
                                TT("pool", xim[:, c0:T], a3[:, c0:T], a4[:, c0:T], ALU.add, r=[a3, a4], w=[xim])
                                MM(yb[:, c0:T], Cf[0][:, sp, :], xre[:, c0:T], s_ == 0, False, r=[Cf[0], xre], w=[yb])
                                MM(yb[:, c0:T], Cf[1][:, sp, :], xim[:, c0:T], False, s_ == 3, r=[Cf[1], xim], w=[yb])
                        if own:
                            yv = yvr.next()
                            STT("dve", yv[:, c0:T], u[:, q, c0:T], s5d[:, q:q + 1], yb[:, c0:T], ALU.mult, ALU.add, r=[u, s5d, yb], w=[yv])
                            i1 = tr.next()
                            ACT(i1[:, c0:T], yv[:, c0:T], AF.Square, r=[yv], w=[i1])
                            TS("dve", i1[:, c0:T], i1[:, c0:T], 0.044715, 1.0, ALU.mult, ALU.add, r=[i1], w=[i1])
                            TT("pool", i1[:, c0:T], i1[:, c0:T], yv[:, c0:T], ALU.mult, r=[i1, yv], w=[i1])
                            ACT(i1[:, c0:T], i1[:, c0:T], AF.Sigmoid, r=[i1], w=[i1], scale=1.5957691216057308)
                            TT("pool", g4[:, q, c0:T], yv[:, c0:T], i1[:, c0:T], ALU.mult, r=[yv, i1], w=[g4])
                    if own:
                        for m in range(4):
                            zb = pr.next()
                            for q in range(4):
                                MM(zb[:, c0:T], Wgl[:, q, m * 128:(m + 1) * 128], g4[:, q, c0:T], q == 0, q == 3, r=[Wgl, g4], w=[zb])
                            z1 = tr.next()
                            ACT(z1[:, c0:T], zb[:, c0:T], AF.Sigmoid, r=[zb, bglu], w=[z1], bias=bglu[:, m:m + 1])
                            so = sor.next()
                            TT("pool", so[:, c0:T], g4[:, m, c0:T], z1[:, c0:T], ALU.mult, r=[g4, z1], w=[so])
                            DMA("sp", ST_v[:, m, e0:e0 + n], so[:, c0:T], r=[so])
                P.barrier()
                P.emit("ph2")
        BLOCKS = [(0, 128)] + [(128 + 512 * i, 512) for i in range(8)]
        if stop_after >= 3:
            with ExitStack() as ps:
                Kh = sb(ps, "Kh", [128, WIN], BF)
                Vh = sb(ps, "Vh", [128, 128, 130], BF)
                Qh = sb(ps, "Qh", [128, NE], BF)
                ptr = Ring([sb(ps, f"pt{i}", [128, T], BF) for i in range(4)])
                onr = [sb(ps, f"on{i}", [128, 4, 128], F32) for i in range(2)]
                dif = sb(ps, "dif", [128, 4, 128], F32)
                junk = sb(ps, "junk", [128, 128], F32)
                sc = Ring([sb(ps, f"sc{i}", [128, 4], F32) for i in range(4)])
                otm = Ring([sb(ps, f"otm{i}", [128, 128], BF) for i in range(2)])
                otr = Ring([sb(ps, f"otr{i}", [128, T], BF) for i in range(2)])
                sring = Ring(banks[0:3])
                tbank = banks[3]
                tb_bf = tbank[:].bitcast(BF)
                oacc = [Buf(banks[4 + s][:, 0:130]) for s in range(4)]
                VS_v = VS.rearrange("k p e -> p k e")
                for h in range(8):
                    for i in range(4):
                        DMA("sp", Kh[:, i * 4096:(i + 1) * 4096], KT[h, :, i * 4096:(i + 1) * 4096], w=[Kh])
                    for i in range(8):
                        DMA("sp", Vh[:, i * 16:(i + 1) * 16, :], VS_v[:, i * 16:(i + 1) * 16, h * 130:(h + 1) * 130], w=[Vh])
                    DMA("sp", Qh[:], QT[h, :, :], w=[Qh])
                    for (e0, Wd) in BLOCKS:
                        qs = OWN0 + e0
                        nsub = Wd // 128
                        nk = (qs + Wd) // 128
                        for c in range(2):
                            accs = oacc
                            def emit_s(kt):
                                col0 = max(kt * 128 - qs, 0)
                                S = sring.next()
                                MM(S[:, col0:Wd], Kh[64 * c:64 * c + 64, kt * 128:(kt + 1) * 128], Qh[64 * c:64 * c + 64, e0 + col0:e0 + Wd], True, True, r=[Kh, Qh], w=[S])
                                return S
                            S_next = emit_s(0)
                            for kt in range(nk):
                                col0 = max(kt * 128 - qs, 0)
                                S = S_next
                                if kt + 1 < nk:
                                    S_next = emit_s(kt + 1)
                                Pt = ptr.next()
                                ACT(Pt[:, col0:Wd], S[:, col0:Wd], AF.Exp, r=[S], w=[Pt])
                                if kt * 128 >= qs:
                                    TT("pool", Pt[:, col0:col0 + 128], Pt[:, col0:col0 + 128], tri_b[:], ALU.mult, r=[Pt, tri_b], w=[Pt])
                                for s in range(col0 // 128, nsub):
                                    MM(accs[s][:, :], Pt[:, s * 128:(s + 1) * 128], Vh[:, kt, :], kt == 0, kt == qs // 128 + s, r=[Pt, Vh], w=[accs[s]])
                            sc_ = sc.next()
                            for s in range(nsub):
                                TS("dve", sc_[:, s:s + 1], accs[s][:, 128:129], 1e-30, None, ALU.add, None, r=[accs[s]], w=[sc_])
                            RECIP(sc_[:, 0:nsub], sc_[:, 0:nsub], r=[sc_], w=[sc_])
                            for s in range(nsub):
                                TS("dve", onr[c][:, s, :], accs[s][:, 0:128], sc_[:, s:s + 1], None, ALU.mult, None, r=[accs[s], sc_], w=[onr[c]])
                        STT("dve", dif[:, 0:nsub, :], onr[1][:, 0:nsub, :], neglam[:, 0:1], onr[0][:, 0:nsub, :], ALU.mult, ALU.add, r=[onr[0], onr[1], neglam], w=[dif])
                        sc_ = sc.next()
                        P.op("dve", lambda e, sc_=sc_: e.memset(sc_[:], 0.0), w=[sc_])
                        for s in range(nsub):
                            P.op("act", lambda e, s=s, sc_=sc_: e.activation(out=junk[:], in_=dif[:, s, :], func=AF.Square, accum_out=sc_[:, s:s + 1]), r=[dif], w=[junk, sc_])
                        ACT(sc_[:, 0:nsub], sc_[:, 0:nsub], AF.Sqrt, r=[sc_], w=[sc_], scale=1.0 / 128.0, bias=epsb[:])
                        RECIP(sc_[:, 0:nsub], sc_[:, 0:nsub], r=[sc_], w=[sc_])
                        TS("dve", sc_[:, 0:nsub], sc_[:, 0:nsub], 1.0 - LAM_INIT, None, ALU.mult, None, r=[sc_], w=[sc_])
                        ot = otr.next()
                        for s in range(nsub):
                            om = otm.next()
                            STT("dve", om[:], dif[:, s, :], sc_[:, s:s + 1], subg[:], ALU.mult, ALU.mult, r=[dif, sc_, subg], w=[om])
                            P.op("pe", lambda e, om=om: e.transpose(tb_bf[:, 0:128], om[:], ident_b[:]), r=[om, ident_b], w=[tbank])
                            ACT(ot[:, s * 128:(s + 1) * 128], tb_bf[:, 0:128], AF.Copy, r=[tbank], w=[ot])
                        DMA("sp", OT[h * 128:(h + 1) * 128, e0:e0 + Wd], ot[:, 0:Wd], r=[ot])
                P.barrier()
                P.emit("ph3")
        xT_v = xT.rearrange("(c p) t -> p c t", p=128)
        OT_v = OT.rearrange("(c p) t -> p c t", p=128)
        ST_v = ST.rearrange("(c p) t -> p c t", p=128)
        GT_v = GT.rearrange("(c p) t -> p c t", p=128)
        HT_v = HT.rearrange("(c p) t -> p c t", p=128)
        HNT_v = HNT.rearrange("(c p) t -> p c t", p=128)
        yT_v = yT.rearrange("(c p) t -> p c t", p=128)
        if stop_after >= 4:
            with ExitStack() as ps:
                Wap = sb(ps, "Wap", [128, 8, 1024], BF)
                Wsp = sb(ps, "Wsp", [128, 4, 1024], BF)
                Wout = sb(ps, "Wout", [128, 8, 1024], BF)
                DMA("pool", Wap[:], w_ap.rearrange("(c p) n -> p c n", p=128), w=[Wap])
                DMA("pool", Wsp[:], w_sp.rearrange("(c p) n -> p c n", p=128), w=[Wsp])
                DMA("pool", Wout[:], w_out.rearrange("(c p) n -> p c n", p=128), w=[Wout])
                o_r = Ring([sb(ps, f"o{i}", [128, 8, T], BF) for i in range(2)])
                s_r = Ring([sb(ps, f"s{i}", [128, 4, T], BF) for i in range(2)])
                g_r = Ring([sb(ps, f"g{i}", [128, 16, T], BF) for i in range(2)])
                x_r = Ring([sb(ps, f"xa{i}", [128, 8, T], F32) for i in range(2)])
                mix = sb(ps, "mix", [128, 8, T], BF)
                hh = Ring([sb(ps, f"h{i}", [128, 8, T], F32) for i in range(2)])
                hn_r = Ring([sb(ps, f"hn{i}", [128, 8, T], BF) for i in range(2)])
                sq = sb(ps, "sq4", [128, 8, T], BF)
                t_r = Ring([sb(ps, f"ta{i}", [128, T], F32) for i in range(4)])
                sd_r = Ring([sb(ps, f"sda{i}", [128, T], F32) for i in range(2)])
                pr = Ring(banks)
                for (e0, Wd) in BLOCKS:
                    o, s_, gt, x = o_r.next(), s_r.next(), g_r.next(), x_r.next()
                    DMA("sp", o[:, :, 0:Wd], OT_v[:, :, e0:e0 + Wd], w=[o])
                    DMA("sp", s_[:, :, 0:Wd], ST_v[:, :, e0:e0 + Wd], w=[s_])
                    DMA("sp", gt[:, :, 0:Wd], GT_v[:, :, e0:e0 + Wd], w=[gt])
                    DMA("sp", x[:, :, 0:Wd], xT_v[:, :, OWN0 + e0:OWN0 + e0 + Wd], w=[x])
                    for m in range(8):
                        ab = pr.next()
                        for c in range(8):
                            MM(ab[:, 0:Wd], Wap[:, c, m * 128:(m + 1) * 128], o[:, c, 0:Wd], c == 0, c == 7, r=[Wap, o], w=[ab])
                        sbk = pr.next()
                        for c in range(4):
                            MM(sbk[:, 0:Wd], Wsp[:, c, m * 128:(m + 1) * 128], s_[:, c, 0:Wd], c == 0, c == 3, r=[Wsp, s_], w=[sbk])
                        t1, t2 = t_r.next(), t_r.next()
                        TT("dve", t1[:, 0:Wd], ab[:, 0:Wd], gt[:, m, 0:Wd], ALU.mult, r=[ab, gt], w=[t1])
                        TT("dve", t2[:, 0:Wd], sbk[:, 0:Wd], gt[:, 8 + m, 0:Wd], ALU.mult, r=[sbk, gt], w=[t2])
                        TT("pool", mix[:, m, 0:Wd], t1[:, 0:Wd], t2[:, 0:Wd], ALU.add, r=[t1, t2], w=[mix])
                    h_ = hh.next()
                    for m in range(8):
                        db = pr.next()
                        for c in range(8):
                            MM(db[:, 0:Wd], Wout[:, c, m * 128:(m + 1) * 128], mix[:, c, 0:Wd], c == 0, c == 7, r=[Wout, mix], w=[db])
                        TT("dve", h_[:, m, 0:Wd], x[:, m, 0:Wd], db[:, 0:Wd], ALU.add, r=[x, db], w=[h_])
                    ACT(sq[:, :, 0:Wd], h_[:, :, 0:Wd], AF.Square, r=[h_], w=[sq])
                    ssb = pr.next()
                    for c in range(8):
                        MM(ssb[:, 0:Wd], ones_bf[:], sq[:, c, 0:Wd], c == 0, c == 7, r=[ones_bf, sq], w=[ssb])
                    sd = sd_r.next()
                    ACT(sd[:, 0:Wd], ssb[:, 0:Wd], AF.Sqrt, r=[ssb], w=[sd], scale=1.0 / D, bias=epsb[:])
                    RECIP(sd[:, 0:Wd], sd[:, 0:Wd], r=[sd], w=[sd])
                    hn = hn_r.next()
                    for c in range(8):
                        STT("dve", hn[:, c, 0:Wd], h_[:, c, 0:Wd], g2[:, c:c + 1], sd[:, 0:Wd], ALU.mult, ALU.mult, r=[h_, g2, sd], w=[hn])
                    DMA("sp", HT_v[:, :, e0:e0 + Wd], h_[:, :, 0:Wd], r=[h_])
                    DMA("sp", HNT_v[:, :, e0:e0 + Wd], hn[:, :, 0:Wd], r=[hn])
                P.barrier()
                P.emit("ph4a")

        if stop_after >= 5:
            with ExitStack() as ps:
                TB = 256
                Wup = sb(ps, "Wup", [128, 8, 5632], BF)
                Wdn = sb(ps, "Wdn", [128, 22, 1024], BF)
                w_up_v = w_up.rearrange("(c p) n -> p c n", p=128)
                for c in range(8):
                    DMA("pool", Wup[:, c, :], w_up_v[:, c, :], w=[Wup])
                DMA("pool", Wdn[:], w_down.rearrange("(j p) n -> p j n", p=128), w=[Wdn])
                convw = sb(ps, "convw", [128, 44, 3], F32)
                convb = sb(ps, "convb", [128, 44], F32)
                carry = sb(ps, "carry", [128, 44, 2], F32)
                DMA("sp", convw[:], convw_d[:, :, :], w=[convw])
                DMA("sp", convb[:], convb_d[:, :], w=[convb])
                P.op("dve", lambda e: e.memset(carry[:], 0.0), w=[carry])
                hn_r = Ring([sb(ps, f"hnb{i}", [128, 8, TB], BF) for i in range(2)])
                hr_r = Ring([sb(ps, f"hrb{i}", [128, 8, TB], F32) for i in range(2)])
                ub_r = Ring([sb(ps, f"ubf{i}", [128, 2 + TB], F32) for i in range(4)])
                a_r = Ring([sb(ps, f"aa{i}", [128, TB], F32) for i in range(4)])
                sg_r = Ring([sb(ps, f"sg{i}", [128, TB], F32) for i in range(2)])
                acts = sb(ps, "acts", [128, 22, TB], BF)
                out_r = Ring([sb(ps, f"ob{i}", [128, 8, TB], F32) for i in range(1)])
                pr = Ring(banks[0:4])
                dr = Ring(banks[4:8])
                tiles = [(0, 128, True)] + [(128 + TB * i, TB, False) for i in range(4096 // TB)]
                for (e0, Wd, halo) in tiles:
                    hn = hn_r.next()
                    DMA("sp", hn[:, :, 0:Wd], HNT_v[:, :, e0:e0 + Wd], w=[hn])
                    if not halo:
                        hres = hr_r.next()
                        DMA("sp", hres[:, :, 0:Wd], HT_v[:, :, e0:e0 + Wd], w=[hres])
                    for j in range(22):
                        av = []
                        for mt in (j, 22 + j):
                            ub = pr.next()
                            for c in range(8):
                                MM(ub[:, 0:Wd], Wup[:, c, mt * 128:(mt + 1) * 128], hn[:, c, 0:Wd], c == 0, c == 7, r=[Wup, hn], w=[ub])
                            if halo:
                                ACT(carry[:, mt, :], ub[:, Wd - 2:Wd], AF.Copy, r=[ub], w=[carry])
                                continue
                            uf = ub_r.next()
                            COPY("pool", uf[:, 0:2], carry[:, mt, :], r=[carry], w=[uf])
                            ACT(uf[:, 2:2 + Wd], ub[:, 0:Wd], AF.Copy, r=[ub], w=[uf])
                            COPY("pool", carry[:, mt, :], uf[:, Wd:Wd + 2], r=[uf], w=[carry])
                            a = a_r.next()
                            TS("dve", a[:, 0:Wd], uf[:, 2:2 + Wd], convw[:, mt, 2:3], convb[:, mt:mt + 1], ALU.mult, ALU.add, r=[uf, convw, convb], w=[a])
                            STT("dve", a[:, 0:Wd], uf[:, 1:1 + Wd], convw[:, mt, 1:2], a[:, 0:Wd], ALU.mult, ALU.add, r=[uf, convw, a], w=[a])
                            STT("dve", a[:, 0:Wd], uf[:, 0:Wd], convw[:, mt, 0:1], a[:, 0:Wd], ALU.mult, ALU.add, r=[uf, convw, a], w=[a])
                            av.append(a)
                        if halo:
                            continue
                        sg = sg_r.next()
                        ACT(sg[:, 0:Wd], av[0][:, 0:Wd], AF.Silu, r=[av[0]], w=[sg])
                        TT("pool", acts[:, j, 0:Wd], sg[:, 0:Wd], av[1][:, 0:Wd], ALU.mult, r=[sg, av[1]], w=[acts])
                    if halo:
                        continue
                    ob = out_r.next()
                    for m in range(8):
                        db = dr.next()
                        for j in range(22):
                            MM(db[:, 0:Wd], Wdn[:, j, m * 128:(m + 1) * 128], acts[:, j, 0:Wd], j == 0, j == 21, r=[Wdn, acts], w=[db])
                        TT("dve", ob[:, m, 0:Wd], hres[:, m, 0:Wd], db[:, 0:Wd], ALU.add, r=[hres, db], w=[ob])
                    DMA("sp", yT_v[:, :, e0 - 128:e0 - 128 + Wd], ob[:, :, 0:Wd], r=[ob])
                P.barrier()
                P.emit("ph4b")

        P.barrier()
        P.emit("fin")
    return nc


def _rep(v, n=128):
    return np.ascontiguousarray(np.broadcast_to(np.asarray(v, np.float32)[None], (n,) + np.asarray(v).shape))


def prep_inputs(inp):
    f = np.float32
    x = np.asarray(inp["x"], f)
    common = {}
    common["ident"] = np.eye(128, dtype=f)
    perm = np.zeros((128, 128), f)
    for dp in range(128):
        blk, d = divmod(dp, 64)
        perm[blk * 64 + (d + 32) % 64, dp] = 1.0
    common["perm"] = perm
    common["tri"] = np.triu(np.ones((128, 128), f))
    common["jrow"] = _rep(np.arange(1, T + 1, dtype=f))
    common["g1"] = np.ascontiguousarray(np.asarray(inp["norm1_gain"], f)[0].reshape(8, 128).T)
    common["g2"] = np.ascontiguousarray(np.asarray(inp["norm2_gain"], f)[0].reshape(8, 128).T)
    gq = np.tile(np.asarray(inp["q_norm_gain"], f)[0], 2)
    gk = np.tile(np.asarray(inp["k_norm_gain"], f)[0], 2)
    common["gqk"] = np.ascontiguousarray(np.stack([gq, gk], axis=1))
    lam4 = np.stack([np.asarray(inp[k], f)[0] for k in ("lambda_q1", "lambda_k1", "lambda_q2", "lambda_k2")])
    common["lamp"] = _rep(lam4)
    common["subg"] = _rep(np.asarray(inp["subln_gain"], f)[0])
    common["w_in"] = np.ascontiguousarray(np.asarray(inp["w_in"], f)[0])
    common["w_ap"] = np.ascontiguousarray(np.asarray(inp["w_attn_proj"], f)[0])
    common["w_glu"] = np.ascontiguousarray(np.asarray(inp["w_glu"], f)[0])
    common["bglu"] = np.ascontiguousarray(np.asarray(inp["b_glu"], f)[0].reshape(4, 128).T)
    common["w_sp"] = np.ascontiguousarray(np.asarray(inp["w_ssm_proj"], f)[0])
    common["w_out"] = np.ascontiguousarray(np.asarray(inp["w_out"], f)[0])
    common["w_up"] = np.ascontiguousarray(np.asarray(inp["w_up"], f)[0])
    cw = np.asarray(inp["conv_w"], f)[0]
    common["convw"] = np.ascontiguousarray(cw.reshape(3, 44, 128).transpose(2, 1, 0))
    common["convb"] = np.ascontiguousarray(np.asarray(inp["conv_b"], f)[0].reshape(44, 128).T)
    common["w_down"] = np.ascontiguousarray(np.asarray(inp["w_down"], f)[0])

    def st(a):
        a = np.asarray(a, f)
        a = a.reshape((16, 2, 64) + a.shape[2:])
        a = np.moveaxis(a, 0, 2)
        return np.ascontiguousarray(a.reshape((128, 16) + a.shape[3:]))
    a_re = np.asarray(inp["ssm_a_re"], f)[0]
    a_im = np.asarray(inp["ssm_a_im"], f)[0]
    ldt = np.broadcast_to(np.asarray(inp["ssm_log_dt"], f)[0][:, None], (32, 64))
    common["s5a"] = np.ascontiguousarray(np.stack([st(a_re), st(a_im), st(ldt)], axis=1))
    common["s5b"] = np.ascontiguousarray(np.stack([st(np.asarray(inp["ssm_b_re"], f)[0]), st(np.asarray(inp["ssm_b_im"], f)[0])], axis=1))
    cre = np.asarray(inp["ssm_c_re"], f)[0].transpose(0, 2, 1)
    cim = np.asarray(inp["ssm_c_im"], f)[0].transpose(0, 2, 1)
    common["s5c"] = np.ascontiguousarray(np.stack([st(cre), st(cim)], axis=1))
    common["s5d"] = np.ascontiguousarray(np.asarray(inp["ssm_d"], f)[0].reshape(4, 128).T)

    inv_freq = (1.0 / (10000.0 ** (np.arange(0, 64, 2, dtype=f) / f(64)))).astype(f)
    maps = []
    for core in range(8):
        b, r = divmod(core, 4)
        w0 = 4096 * (r - 3)
        pos = np.arange(w0, w0 + WIN)
        vmask = pos >= 0
        xw = np.zeros((WIN, D), f)
        xw[vmask] = x[b, pos[vmask]]
        ang = (np.where(vmask, pos, 0).astype(f)[:, None] * inv_freq[None, :]).astype(f)
        cos = np.cos(ang).astype(f).T
        sin = np.sin(ang).astype(f).T
        m = dict(common)
        m["xT"] = np.ascontiguousarray(xw.T)
        m["cosT"] = np.ascontiguousarray(np.concatenate([cos, cos, cos, cos], axis=0))
        m["sinT"] = np.ascontiguousarray(np.concatenate([-sin, sin, -sin, sin], axis=0))
        m["validT"] = np.ascontiguousarray(vmask.astype(f).reshape(128, 128).T)
        maps.append(m)
    return maps


_NC_CACHE = {}


def kernel(**inputs):
    maps = prep_inputs(inputs)
    if "nc" not in _NC_CACHE:
        _NC_CACHE["nc"] = build_nc()
    res = run_bass_kernel_spmd(_NC_CACHE["nc"], maps, core_ids=list(range(8)))
    out = np.zeros((2, 16384, D), np.float32)
    for core in range(8):
        b, r = divmod(core, 4)
        out[b, 4096 * r:4096 * (r + 1), :] = np.asarray(res.results[core]["yT"], np.float32).T
    return out
```
